# Optimizing a Trainium2 kernel written in Bass

```python
import math
import jax, jax.numpy as jnp
from jax import lax
import numpy as np

D_MODEL = 1024
BATCH = 8
SEQ = 4096
DEPTH = 4

GRID_W = 64
CTX_LEN = 256
N_MIXERS = 4
EPS = 1e-6
Q_BLOCK = 128
N_MOD = 6

MLA_HEADS = 16
MLA_Q_RANK = 384
MLA_KV_RANK = 256
MLA_NOPE_DIM = 64
MLA_ROPE_DIM = 32
MLA_V_DIM = 64
MLA_QK_DIM = MLA_NOPE_DIM + MLA_ROPE_DIM
ROPE_THETA = 10000.0

S5_WIDTH = D_MODEL // 2
S5_GROUP_CH = 16
S5_GROUPS = S5_WIDTH // S5_GROUP_CH
S5_STATE = 64
S5_DT_MIN = 1e-3
S5_DT_MAX = 1e-1

HG_HEAD_DIM = 128
HG_HEADS = D_MODEL // HG_HEAD_DIM
HG_CHUNK = 64

NA_HEADS = 16
NA_HEAD_DIM = D_MODEL // NA_HEADS
NA_KH = 8
NA_KW = 16

FFN_DIM = ((8 * D_MODEL // 3 + 255) // 256) * 256

kernel_name = "hybrid_interleaved_diffusion_trunk"

F32 = jnp.float32


def rms_norm(x, g):
    xf = x.astype(F32)
    y = xf * lax.rsqrt(jnp.mean(xf * xf, axis=-1, keepdims=True) + EPS)
    return (y * g.astype(F32)).astype(x.dtype)


def modulate(x, shift, scale):
    return x * (1 + scale) + shift


def _flip(t):
    return jnp.flip(t, axis=1)


def _ident(t):
    return t


def blocked_attention(q, k, v, scale):
    bsz, lq, h, dk = q.shape
    qb = jnp.moveaxis(q.reshape(bsz, lq // Q_BLOCK, Q_BLOCK, h, dk), 1, 0)

    def attend(qblk):
        s = jnp.einsum("bqhd,bkhd->bhqk", qblk, k, preferred_element_type=F32) * scale
        p = jax.nn.softmax(s, axis=-1).astype(v.dtype)
        return jnp.einsum("bhqk,bkhd->bqhd", p, v)

    o = lax.map(attend, qb)
    return jnp.moveaxis(o, 0, 1).reshape(bsz, lq, h, v.shape[-1])


def rope_rotate(v, pos):
    half = v.shape[-1] // 2
    inv = 1.0 / (ROPE_THETA ** (jnp.arange(half, dtype=F32) / half))
    ang = pos.astype(F32)[:, None] * inv[None, :]
    cos = jnp.cos(ang)[None, :, None, :]
    sin = jnp.sin(ang)[None, :, None, :]
    vf = v.astype(F32)
    v1, v2 = vf[..., :half], vf[..., half:]
    return jnp.concatenate([v1 * cos - v2 * sin, v1 * sin + v2 * cos], axis=-1).astype(v.dtype)


def axial_rope_2d(x):
    t = jnp.arange(x.shape[1])
    a = MLA_ROPE_DIM // 2
    return jnp.concatenate([rope_rotate(x[..., :a], t // GRID_W),
                            rope_rotate(x[..., a:], t % GRID_W)], axis=-1)


def mla_project(u, w_in, q_norm_g, w_q_up, kv_norm_g, w_kv_up):
    bsz, n, _ = u.shape
    lat = u @ w_in
    q_lat = lat[..., :MLA_Q_RANK]
    kv_lat = lat[..., MLA_Q_RANK:MLA_Q_RANK + MLA_KV_RANK]
    k_rope = lat[..., MLA_Q_RANK + MLA_KV_RANK:][:, :, None, :]
    q = (rms_norm(q_lat, q_norm_g) @ w_q_up).reshape(bsz, n, MLA_HEADS, MLA_QK_DIM)
    kv = (rms_norm(kv_lat, kv_norm_g) @ w_kv_up).reshape(bsz, n, MLA_HEADS, MLA_NOPE_DIM + MLA_V_DIM)
    return q[..., :MLA_NOPE_DIM], q[..., MLA_NOPE_DIM:], kv[..., :MLA_NOPE_DIM], k_rope, kv[..., MLA_NOPE_DIM:]


def mla_mixer(u, uc, w_in, q_norm_g, w_q_up, kv_norm_g, w_kv_up, w_out, with_ctx_out):
    bsz, n, _ = u.shape
    qn, qr, kn, kr, v = mla_project(u, w_in, q_norm_g, w_q_up, kv_norm_g, w_kv_up)
    qr, kr = axial_rope_2d(qr), axial_rope_2d(kr)
    q = jnp.concatenate([qn, qr], axis=-1)
    k = jnp.concatenate([kn, jnp.broadcast_to(kr, qr.shape)], axis=-1)
    qnc, qrc, knc, krc, vc = mla_project(uc, w_in, q_norm_g, w_q_up, kv_norm_g, w_kv_up)
    kc = jnp.concatenate([knc, jnp.broadcast_to(krc, qrc.shape)], axis=-1)
    scale = MLA_QK_DIM ** -0.5
    o = blocked_attention(q, jnp.concatenate([kc, k], axis=1), jnp.concatenate([vc, v], axis=1), scale)
    y = o.reshape(bsz, n, MLA_HEADS * MLA_V_DIM) @ w_out
    yc = None
    if with_ctx_out:
        oc = blocked_attention(jnp.concatenate([qnc, qrc], axis=-1), kc, vc, scale)
        yc = oc.reshape(bsz, uc.shape[1], MLA_HEADS * MLA_V_DIM) @ w_out
    return y, yc


def s5_discretize(lam_re, lam_im, log_dt, b_re, b_im):
    lam = lax.complex(lam_re.astype(F32), lam_im.astype(F32))
    dt = jnp.exp(log_dt.astype(F32))[:, None]
    lam_bar = jnp.exp(lam * dt)
    b_mat = lax.complex(b_re.astype(F32), b_im.astype(F32))
    b_bar = ((lam_bar - 1.0) / lam)[..., None] * b_mat
    return lam_bar, b_bar


def diagonal_scan(lam_bar, bu, h0):
    bu = bu.at[:, 0].add(lam_bar * h0)
    a = jnp.broadcast_to(lam_bar, (1,) + bu.shape[1:])

    def combine(left, right):
        a_l, b_l = left
        a_r, b_r = right
        return a_r * a_l, a_r * b_l + b_r

    return lax.associative_scan(combine, (a, bu), axis=1)[1]


def s5_mixer(u, uc, w_in, lam_re, lam_im, log_dt, b_re, b_im, c_re, c_im, d_skip, w_glu, with_ctx_out):
    z = (u @ w_in).astype(F32)
    zc = (uc @ w_in).astype(F32)

    def grp(t):
        return t.reshape(t.shape[0], t.shape[1], S5_GROUPS, S5_GROUP_CH)

    dsk = d_skip.astype(F32)
    y = z * dsk
    yc = zc * dsk
    for d in range(2):
        fl = _flip if d == 1 else _ident
        lam_bar, b_bar = s5_discretize(lam_re[d], lam_im[d], log_dt[d], b_re[d], b_im[d])
        c_mat = lax.complex(c_re[d].astype(F32), c_im[d].astype(F32))
        h0 = jnp.zeros((zc.shape[0], S5_GROUPS, S5_STATE), jnp.complex64)
        hc = diagonal_scan(lam_bar, jnp.einsum("gnc,blgc->blgn", b_bar, grp(fl(zc))), h0)
        h = diagonal_scan(lam_bar, jnp.einsum("gnc,blgc->blgn", b_bar, grp(fl(z))), hc[:, -1])
        y = y + fl(jnp.real(jnp.einsum("gcn,blgn->blgc", c_mat, h))).reshape(z.shape)
        if with_ctx_out:
            yc = yc + fl(jnp.real(jnp.einsum("gcn,blgn->blgc", c_mat, hc))).reshape(zc.shape)

    def glu(t):
        a, g = jnp.split(jax.nn.gelu(t).astype(u.dtype) @ w_glu, 2, axis=-1)
        return a * jax.nn.sigmoid(g)

    return glu(y), (glu(yc) if with_ctx_out else None)


def gla_chunkwise(q, k, v, log_f, s0):
    bsz, n, h, dk = q.shape
    dv = v.shape[-1]
    nc = n // HG_CHUNK

    def blk(t):
        return t.reshape(bsz, nc, HG_CHUNK, h, t.shape[-1])

    q, k, v, log_f = blk(q), blk(k), blk(v), blk(log_f)
    b = jnp.cumsum(log_f, axis=2)
    b_last = b[:, :, -1:]
    q_in = q * jnp.exp(b)
    k_in = k * jnp.exp(-b)
    k_out = k * jnp.exp(b_last - b)
    att = jnp.einsum("bcthd,bcshd->bchts", q_in, k_in)
    incl = jnp.tril(jnp.ones((HG_CHUNK, HG_CHUNK), dtype=bool))
    att = jnp.where(incl, att, 0.0)
    o = jnp.einsum("bchts,bcshv->bcthv", att, v)
    ds = jnp.einsum("bcshd,bcshv->cbhdv", k_out, v)
    decay = jnp.moveaxis(jnp.exp(b_last[:, :, 0]), 1, 0)

    def step(s, inp):
        dec, d_s = inp
        return dec[..., None] * s + d_s, s

    s_final, s_start = lax.scan(step, s0, (decay, ds))
    o = o + jnp.einsum("bcthd,cbhdv->bcthv", q_in, s_start)
    return o.reshape(bsz, n, h, dv), s_final


def hgrn2_mixer(u, uc, w_in, lower_bound, norm_g, w_out, with_ctx_out):
    lb = lower_bound.astype(F32).reshape(HG_HEADS, HG_HEAD_DIM)

    def prep(t):
        z = (t @ w_in).astype(F32).reshape(t.shape[0], t.shape[1], 5, HG_HEADS, HG_HEAD_DIM)
        q = jax.nn.silu(z[:, :, 0]) * HG_HEAD_DIM ** -0.5
        forget = lb + (1.0 - lb) * jax.nn.sigmoid(z[:, :, 1:3])
        return q, 1.0 - forget, jnp.log(forget), z[:, :, 3], z[:, :, 4]

    q, k, log_f, v, g = prep(u)
    qc, kc, log_fc, vc, gc = prep(uc)
    o = jnp.zeros_like(v)
    oc = jnp.zeros_like(vc)
    for d in range(2):
        fl = _flip if d == 1 else _ident
        s0 = jnp.zeros((uc.shape[0], HG_HEADS, HG_HEAD_DIM, HG_HEAD_DIM), F32)
        oc_d, s_ctx = gla_chunkwise(fl(qc), fl(kc[:, :, d]), fl(vc), fl(log_fc[:, :, d]), s0)
        o_d, _ = gla_chunkwise(fl(q), fl(k[:, :, d]), fl(v), fl(log_f[:, :, d]), s_ctx)
        o = o + fl(o_d)
        if with_ctx_out:
            oc = oc + fl(oc_d)

    def readout(o_h, g_h):
        on = o_h * lax.rsqrt(jnp.mean(o_h * o_h, axis=-1, keepdims=True) + EPS)
        shp = (o_h.shape[0], o_h.shape[1], D_MODEL)
        on = on.reshape(shp) * norm_g.astype(F32) * jax.nn.silu(g_h.reshape(shp))
        return on.astype(u.dtype) @ w_out

    return readout(o, g), (readout(oc, gc) if with_ctx_out else None)


def natten_mixer(u, uc, w_qkv, rpb, w_out, with_ctx_out):
    bsz, n, _ = u.shape
    rows = n // GRID_W
    kh = min(NA_KH, rows)
    n_loc = kh * NA_KW

    def heads(t):
        return t.reshape(t.shape[0], t.shape[1], NA_HEADS, NA_HEAD_DIM)

    q, k, v = (heads(t) for t in jnp.split(u @ w_qkv, 3, axis=-1))
    qc, kc, vc = (heads(t) for t in jnp.split(uc @ w_qkv, 3, axis=-1))
    scale = NA_HEAD_DIM ** -0.5

    def grid(t):
        return t.reshape(bsz, rows, GRID_W, NA_HEADS, NA_HEAD_DIM)

    qg, kg, vg = grid(q), grid(k), grid(v)
    col = jnp.arange(GRID_W)
    col_idx = jnp.clip(col - NA_KW // 2, 0, GRID_W - NA_KW)[:, None] + jnp.arange(NA_KW)[None, :]
    col_off = col_idx - col[:, None] + (NA_KW - 1)

    def row_block(r):
        r0 = jnp.clip(r - kh // 2, 0, rows - kh)
        k_win = lax.dynamic_slice_in_dim(kg, r0, kh, axis=1)[:, :, col_idx]
        v_win = lax.dynamic_slice_in_dim(vg, r0, kh, axis=1)[:, :, col_idx]
        q_row = lax.dynamic_index_in_dim(qg, r, axis=1, keepdims=False)
        row_off = r0 + jnp.arange(kh) - r + (NA_KH - 1)
        bias = rpb[:, row_off[:, None, None], col_off[None, :, :]]
        s_loc = jnp.einsum("bwhd,brwchd->bhwrc", q_row, k_win, preferred_element_type=F32) * scale
        s_loc = s_loc + jnp.transpose(bias, (0, 2, 1, 3))[None].astype(F32)
        s_ctx = jnp.einsum("bwhd,bjhd->bhwj", q_row, kc, preferred_element_type=F32) * scale
        s = jnp.concatenate([s_loc.reshape(bsz, NA_HEADS, GRID_W, n_loc), s_ctx], axis=-1)
        p = jax.nn.softmax(s, axis=-1).astype(v.dtype)
        p_loc = p[..., :n_loc].reshape(bsz, NA_HEADS, GRID_W, kh, NA_KW)
        return (jnp.einsum("bhwrc,brwchd->bwhd", p_loc, v_win)
                + jnp.einsum("bhwj,bjhd->bwhd", p[..., n_loc:], vc))

    o = lax.map(row_block, jnp.arange(rows))
    y = jnp.moveaxis(o, 0, 1).reshape(bsz, n, D_MODEL) @ w_out
    yc = None
    if with_ctx_out:
        yc = blocked_attention(qc, kc, vc, scale).reshape(bsz, uc.shape[1], D_MODEL) @ w_out
    return y, yc


def conv_ffn(u, w_up, conv_w, conv_b, w_down):
    h = u @ w_up
    hp = jnp.pad(h, ((0, 0), (1, 1), (0, 0)))
    h = hp[:, :-2] * conv_w[0] + hp[:, 1:-1] * conv_w[1] + hp[:, 2:] * conv_w[2] + conv_b
    a, g = jnp.split(h, 2, axis=-1)
    return (a * jax.nn.silu(g)) @ w_down


def setup_inputs(seed: int = 0) -> dict:
    key = jax.random.key(seed)
    ks = jax.random.split(key, 36)

    def nrm(i, shape, s):
        return jax.random.normal(ks[i], shape, F32) * s

    def gain(i, shape):
        return 1.0 + nrm(i, shape, 0.02)

    na, nb, nc, nd = (len(range(kind, DEPTH, N_MIXERS)) for kind in range(N_MIXERS))
    n_idx = jnp.arange(S5_STATE, dtype=F32)
    d = D_MODEL
    return {
        "x": nrm(0, (BATCH, SEQ, d), 1.0),
        "c": nrm(1, (BATCH, d), 1.0),
        "ctx": nrm(2, (BATCH, CTX_LEN, d), 1.0),
        "c_ctx": nrm(3, (d,), 1.0),
        "ada_w": nrm(4, (DEPTH, d, N_MOD * d), 0.5 * d ** -0.5),
        "ada_b": nrm(5, (DEPTH, N_MOD * d), 0.02),
        "norm1_g": gain(6, (DEPTH, d)),
        "norm2_g": gain(7, (DEPTH, d)),
        "mla_w_in": nrm(8, (na, d, MLA_Q_RANK + MLA_KV_RANK + MLA_ROPE_DIM), d ** -0.5),
        "mla_q_norm_g": gain(9, (na, MLA_Q_RANK)),
        "mla_w_q_up": nrm(10, (na, MLA_Q_RANK, MLA_HEADS * MLA_QK_DIM), MLA_Q_RANK ** -0.5),
        "mla_kv_norm_g": gain(11, (na, MLA_KV_RANK)),
        "mla_w_kv_up": nrm(12, (na, MLA_KV_RANK, MLA_HEADS * (MLA_NOPE_DIM + MLA_V_DIM)), MLA_KV_RANK ** -0.5),
        "mla_w_out": nrm(13, (na, MLA_HEADS * MLA_V_DIM, d), (MLA_HEADS * MLA_V_DIM) ** -0.5),
        "s5_w_in": nrm(14, (nb, d, S5_WIDTH), d ** -0.5),
        "s5_lambda_re": -0.5 + nrm(15, (nb, 2, S5_GROUPS, S5_STATE), 0.01),
        "s5_lambda_im": math.pi * n_idx + nrm(16, (nb, 2, S5_GROUPS, S5_STATE), 0.01),
        "s5_log_dt": jax.random.uniform(ks[17], (nb, 2, S5_GROUPS), F32,
                                        minval=math.log(S5_DT_MIN), maxval=math.log(S5_DT_MAX)),
        "s5_b_re": nrm(18, (nb, 2, S5_GROUPS, S5_STATE, S5_GROUP_CH), (2 * S5_GROUP_CH) ** -0.5),
        "s5_b_im": nrm(19, (nb, 2, S5_GROUPS, S5_STATE, S5_GROUP_CH), (2 * S5_GROUP_CH) ** -0.5),
        "s5_c_re": nrm(20, (nb, 2, S5_GROUPS, S5_GROUP_CH, S5_STATE), S5_STATE ** -0.5),
        "s5_c_im": nrm(21, (nb, 2, S5_GROUPS, S5_GROUP_CH, S5_STATE), S5_STATE ** -0.5),
        "s5_d": nrm(22, (nb, S5_WIDTH), 1.0),
        "s5_w_glu": nrm(23, (nb, S5_WIDTH, 2 * d), S5_WIDTH ** -0.5),
        "hg_w_in": nrm(24, (nc, d, 5 * d), d ** -0.5),
        "hg_lower_bound": nrm(25, (DEPTH, HG_HEADS * HG_HEAD_DIM), 0.1),
        "hg_norm_g": gain(26, (nc, d)),
        "hg_w_out": nrm(27, (nc, d, d), d ** -0.5),
        "na_w_qkv": nrm(28, (nd, d, 3 * d), d ** -0.5),
        "na_rpb": nrm(29, (nd, NA_HEADS, 2 * NA_KH - 1, 2 * NA_KW - 1), 0.1),
        "na_w_out": nrm(30, (nd, d, d), d ** -0.5),
        "ffn_w_up": nrm(31, (DEPTH, d, 2 * FFN_DIM), d ** -0.5),
        "ffn_conv_w": nrm(32, (DEPTH, 3, 2 * FFN_DIM), 3 ** -0.5),
        "ffn_conv_b": nrm(33, (DEPTH, 2 * FFN_DIM), 0.02),
        "ffn_w_down": nrm(34, (DEPTH, FFN_DIM, d), FFN_DIM ** -0.5),
        "final_g": gain(35, (d,)),
    }


def reference(x, c, ctx, c_ctx, ada_w, ada_b, norm1_g, norm2_g,
              mla_w_in, mla_q_norm_g, mla_w_q_up, mla_kv_norm_g, mla_w_kv_up, mla_w_out,
              s5_w_in, s5_lambda_re, s5_lambda_im, s5_log_dt, s5_b_re, s5_b_im, s5_c_re, s5_c_im,
              s5_d, s5_w_glu,
              hg_w_in, hg_lower_bound, hg_norm_g, hg_w_out,
              na_w_qkv, na_rpb, na_w_out,
              ffn_w_up, ffn_conv_w, ffn_conv_b, ffn_w_down,
              final_g):
    lb_cum = jnp.cumsum(jax.nn.softmax(hg_lower_bound.astype(F32), axis=0), axis=0)
    lower_bounds = lb_cum - lb_cum[0]
    cond = jax.nn.silu(c)
    cond_ctx = jax.nn.silu(c_ctx)
    h_ctx = ctx
    for i in range(DEPTH):
        kind, j = i % N_MIXERS, i // N_MIXERS
        with_ctx_out = i < DEPTH - 1
        mod = cond @ ada_w[i] + ada_b[i]
        mod_c = cond_ctx @ ada_w[i] + ada_b[i]
        sh1, sc1, g1, sh2, sc2, g2 = jnp.split(mod[:, None, :], N_MOD, axis=-1)
        csh1, csc1, cg1, csh2, csc2, cg2 = jnp.split(mod_c, N_MOD, axis=-1)
        u = modulate(rms_norm(x, norm1_g[i]), sh1, sc1)
        uc = modulate(rms_norm(h_ctx, norm1_g[i]), csh1, csc1)
        if kind == 0:
            y, yc = mla_mixer(u, uc, mla_w_in[j], mla_q_norm_g[j], mla_w_q_up[j], mla_kv_norm_g[j],
                              mla_w_kv_up[j], mla_w_out[j], with_ctx_out)
        elif kind == 1:
            y, yc = s5_mixer(u, uc, s5_w_in[j], s5_lambda_re[j], s5_lambda_im[j], s5_log_dt[j],
                             s5_b_re[j], s5_b_im[j], s5_c_re[j], s5_c_im[j], s5_d[j], s5_w_glu[j],
                             with_ctx_out)
        elif kind == 2:
            y, yc = hgrn2_mixer(u, uc, hg_w_in[j], lower_bounds[i], hg_norm_g[j], hg_w_out[j], with_ctx_out)
        else:
            y, yc = natten_mixer(u, uc, na_w_qkv[j], na_rpb[j], na_w_out[j], with_ctx_out)
        x = x + g1 * y
        x = x + g2 * conv_ffn(modulate(rms_norm(x, norm2_g[i]), sh2, sc2),
                              ffn_w_up[i], ffn_conv_w[i], ffn_conv_b[i], ffn_w_down[i])
        if with_ctx_out:
            h_ctx = h_ctx + cg1 * yc
            h_ctx = h_ctx + cg2 * conv_ffn(modulate(rms_norm(h_ctx, norm2_g[i]), csh2, csc2),
                                           ffn_w_up[i], ffn_conv_w[i], ffn_conv_b[i], ffn_w_down[i])
    return rms_norm(x, final_g)
```

```python
import contextlib
import numpy as np
import concourse.bass as bass
import concourse.mybir as mybir
from concourse.bass_utils import run_bass_kernel_spmd

F32 = mybir.dt.float32
BF16 = mybir.dt.bfloat16
ALU = mybir.AluOpType
AF = mybir.ActivationFunctionType


class Prog:
    NDS = 8

    def __init__(self):
        self.nc = nc = bass.Bass("TRN2", target_bir_lowering=False)
        self.es = contextlib.ExitStack()
        self.engs = {"pe": nc.tensor, "act": nc.scalar, "dve": nc.vector, "pool": nc.gpsimd, "sp": nc.sync}
        self.sem = {e: self.es.enter_context(nc.semaphore("s_" + e)) for e in self.engs}
        self.cnt = {e: 0 for e in self.engs}
        self.dsem = {q: [self.es.enter_context(nc.semaphore("d_%s%d" % (q, i))) for i in range(self.NDS)]
                     for q in ("sp", "act", "pool")}
        self.dcnt = {q: [0] * self.NDS for q in self.dsem}
        self.dnext = {q: 0 for q in self.dsem}
        self.known = {e: {} for e in self.engs}
        self.track = {}
        self.notrack = set()
        self.ntens = 0
        self.nwaits = 0
        self.ninstr = 0
        self.banks = [self.ps([128, 512], F32, name="bank%d" % i) for i in range(8)]
        self.nbank = 0
        self.rot = list(range(8))

    def sb(self, shape, dt=F32, name=None):
        self.ntens += 1
        return self.es.enter_context(self.nc.sbuf_tensor(name or ("sb%d" % self.ntens), list(shape), dt))

    def ps(self, shape, dt=F32, name=None):
        self.ntens += 1
        return self.es.enter_context(self.nc.psum_tensor(name or ("ps%d" % self.ntens), list(shape), dt))

    def dram(self, name, shape, dt=F32, kind="Internal"):
        t = self.nc.dram_tensor(name, list(shape), dt, kind=kind)
        if kind == "ExternalInput":
            self.notrack.add(name)
        return t

    @staticmethod
    def _region(ap):
        shp = ap.tensor.shape
        rs = 1
        for s in shp[1:]:
            rs *= s
        off = int(ap.offset)
        r0, c0 = off // rs, off % rs
        r1, c1 = r0, c0
        for step, cnt in ap.ap:
            if cnt <= 1 or step == 0:
                continue
            if abs(step) >= rs and step % rs == 0:
                dlt = (cnt - 1) * (step // rs)
                if dlt > 0:
                    r1 += dlt
                else:
                    r0 += dlt
            else:
                dlt = (cnt - 1) * step
                if dlt > 0:
                    c1 += dlt
                else:
                    c0 += dlt
        if ap.name.startswith("bank"):
            return (ap.name, (r0 // 32) * 32, (r1 // 32 + 1) * 32, 0, rs)
        return (ap.name, r0, r1 + 1, c0, c1 + 1)

    @staticmethod
    def _ovl(a, b):
        return a[1] < b[2] and b[1] < a[2] and a[3] < b[4] and b[3] < a[4]

    @staticmethod
    def _contains(a, b):
        return a[1] <= b[1] and b[2] <= a[2] and a[3] <= b[3] and b[4] <= a[4]

    def _deps(self, reads, writes):
        toks = []
        for ap in reads:
            if ap.name in self.notrack:
                continue
            rg = self._region(ap)
            for ent in self.track.get(rg[0], ()):
                if ent["w"] is not None and self._ovl(ent["rg"], rg):
                    toks.append(ent["w"])
        for ap in writes:
            rg = self._region(ap)
            for ent in self.track.get(rg[0], ()):
                if self._ovl(ent["rg"], rg):
                    if ent["w"] is not None:
                        toks.append(ent["w"])
                    toks.extend(ent["r"].items())
        return toks

    def _record(self, reads, writes, tok):
        for ap in writes:
            rg = self._region(ap)
            lst = self.track.setdefault(rg[0], [])
            lst[:] = [e for e in lst if not self._contains(rg, e["rg"])]
            lst.append({"rg": rg, "w": tok, "r": {}})
        for ap in reads:
            if ap.name in self.notrack:
                continue
            rg = self._region(ap)
            lst = self.track.setdefault(rg[0], [])
            for e in lst:
                if e["rg"] == rg:
                    if e["r"].get(tok[0], 0) < tok[1]:
                        e["r"][tok[0]] = tok[1]
                    break
            else:
                lst.append({"rg": rg, "w": None, "r": {tok[0]: tok[1]}})

    def _semh(self, key):
        return self.sem[key[1]] if key[0] == "c" else self.dsem[key[1]][key[2]]

    def _wait(self, eng, toks, skip_pe_self=True):
        best = {}
        for k, v in toks:
            if best.get(k, 0) < v:
                best[k] = v
        kn = self.known[eng]
        for k, v in best.items():
            if eng == "pe" and k == ("c", "pe"):
                continue
            if kn.get(k, 0) >= v:
                continue
            self.engs[eng].wait_ge(self._semh(k), v)
            self.nwaits += 1
            kn[k] = v

    def op(self, eng, fn, reads, writes):
        if eng != "pe":
            pr = [a for a in reads if a.name.startswith("bank")]
            if pr:
                writes = list(writes) + pr
                reads = [a for a in reads if not a.name.startswith("bank")]
        self._wait(eng, self._deps(reads, writes))
        ins = fn(self.engs[eng])
        self.cnt[eng] += 1
        tok = (("c", eng), self.cnt[eng])
        ins.then_inc(self.sem[eng], 1)
        self._record(reads, writes, tok)
        self.ninstr += 1
        return tok

    def dma(self, q, out, in_, **kw):
        i = self.dnext[q]
        self.dnext[q] = (i + 1) % self.NDS
        key = ("d", q, i)
        toks = self._deps([in_], [out])
        if self.dcnt[q][i] > 0:
            toks.append((key, 16 * self.dcnt[q][i]))
        self._wait(q, toks)
        self.engs[q].dma_start(out=out, in_=in_, **kw).then_inc(self.dsem[q][i], 16)
        self.dcnt[q][i] += 1
        tok = (key, 16 * self.dcnt[q][i])
        self._record([in_], [out], tok)
        self.ninstr += 1
        return tok

    def mm(self, out, lhsT, rhs, start=True, stop=True):
        return self.op("pe", lambda e: e.matmul(out, lhsT, rhs, start=start, stop=stop), [lhsT, rhs], [out])

    def actv(self, out, in_, func, bias=None, scale=None, accum_out=None, eng="act"):
        kw = {}
        rd = [in_]
        wr = [out]
        if bias is not None:
            kw["bias"] = bias
            if not isinstance(bias, (int, float)):
                rd.append(bias)
        if scale is not None:
            kw["scale"] = scale
            if not isinstance(scale, (int, float)):
                rd.append(scale)
        if accum_out is not None:
            kw["accum_out"] = accum_out
            wr.append(accum_out)
        return self.op(eng, lambda e: e.activation(out, in_, func, **kw), rd, wr)

    def tt(self, eng, out, in0, in1, op):
        return self.op(eng, lambda e: e.tensor_tensor(out, in0, in1, op), [in0, in1], [out])

    def ts(self, eng, out, in0, s1, s2=None, op0=ALU.mult, op1=None):
        rd = [in0] + [s for s in (s1, s2) if s is not None and not isinstance(s, (int, float))]
        if op1 is None:
            return self.op(eng, lambda e: e.tensor_scalar(out, in0, s1, None, op0), rd, [out])
        return self.op(eng, lambda e: e.tensor_scalar(out, in0, s1, s2, op0, op1), rd, [out])

    def stt(self, eng, out, in0, scalar, in1, op0, op1):
        rd = [in0, in1] + ([] if isinstance(scalar, (int, float)) else [scalar])
        return self.op(eng, lambda e: e.scalar_tensor_tensor(out, in0, scalar, in1, op0, op1), rd, [out])

    def copy(self, eng, out, in_):
        if eng == "act":
            return self.op(eng, lambda e: e.copy(out, in_), [in_], [out])
        return self.op(eng, lambda e: e.tensor_copy(out, in_), [in_], [out])

    def memset(self, eng, ap, val):
        return self.op(eng, lambda e: e.memset(ap, val), [], [ap])

    def scan(self, out, d0, d1, init, op0=ALU.mult, op1=ALU.add):
        rd = [d0, d1] + ([] if isinstance(init, (int, float)) else [init])
        return self.op("dve", lambda e: e.tensor_tensor_scan(out, d0, d1, init, op0, op1), rd, [out])

    def recip(self, out, in_):
        return self.op("dve", lambda e: e.reciprocal(out, in_), [in_], [out])

    def barrier(self):
        toks = [(("c", e), self.cnt[e]) for e in self.engs if self.cnt[e] > 0]
        for q in self.dsem:
            for i in range(self.NDS):
                if self.dcnt[q][i] > 0:
                    toks.append((("d", q, i), 16 * self.dcnt[q][i]))
        for e in self.engs:
            self._wait(e, toks)
        self.track = {}

    @contextlib.contextmanager
    def scope(self):
        outer = self.es
        self.es = contextlib.ExitStack()
        try:
            yield
        finally:
            self.barrier()
            self.es.close()
            self.es = outer

    def bank(self):
        r = self.rot
        b = self.banks[r[self.nbank % len(r)]]
        self.nbank += 1
        return b

    def reserve(self, k):
        self.rot = list(range(k, 8))
        return [self.banks[j] for j in range(k)]

    def unreserve(self):
        self.rot = list(range(8))

    def finish(self, toks):
        self._wait("sp", toks)
        self.es.close()


D = 1024
NCH = 8
TC = 256
TX = 4096
T = TC + TX
TP = T + 3
TILES = [(0, 256)] + [(256 + 512 * j, 512) for j in range(8)]
FFN = 2816
NPAIR = 22
EPS = 1e-6


def pc(t):
    return t + 1 if t < TC else t + 2


def fm(ap2d, lo=None, hi=None):
    v = ap2d.rearrange("(c p) t -> p c t", p=128)
    return v


class Ctx:
    pass


def setup_common(P, G):
    C = Ctx()
    C.ones = P.sb([128, 128], BF16, name="ones_bf")
    P.memset("dve", C.ones[:], 1.0)
    iot = P.sb([128, 128], F32, name="iota_t")
    P.op("pool", lambda e: e.iota(iot[:], [[1, 128]], base=0, channel_multiplier=-1,
                                   allow_small_or_imprecise_dtypes=True), [], [iot[:]])
    C.epsb = P.sb([128, 1], F32, name="epsb")
    P.memset("dve", C.epsb[:], EPS)
    C.iot = iot
    C.ident = P.sb([128, 128], BF16, name="ident_bf")
    P.ts("dve", C.ident[:], iot[:], 0.0, None, ALU.is_equal)
    C.n1g = P.sb([128, 4, 8], F32, name="n1g")
    C.n2g = P.sb([128, 4, 8], F32, name="n2g")
    C.fing = P.sb([128, 8], F32, name="fing")
    C.adab = P.sb([128, 4, 48], F32, name="adab")
    P.dma("sp", C.n1g[:], G["norm1_g"])
    P.dma("sp", C.n2g[:], G["norm2_g"])
    P.dma("sp", C.fing[:], G["final_g"])
    P.dma("sp", C.adab[:], G["ada_b"])
    C.mod = P.sb([128, 4, 48, 2], F32, name="modtab")
    C.gm1 = P.sb([128, 4, 8, 2], F32, name="gm1")
    C.gm2 = P.sb([128, 4, 8, 2], F32, name="gm2")
    with P.scope():
        cond = P.sb([128, 8, 2], F32)
        condb = P.sb([128, 8, 2], BF16)
        P.dma("sp", cond[:], G["cond2"])
        P.actv(condb[:], cond[:], AF.Silu)
        wblk = [P.sb([128, 8, 1024], BF16) for _ in range(2)]
        nb = 0
        for i in G["layers"]:
            ps = P.bank()
            for blk in range(6):
                w = wblk[nb % 2]
                nb += 1
                src = G["ada_w"][i, :, blk * 1024:(blk + 1) * 1024].rearrange("(c p) n -> p c n", p=128)
                P.dma("pool", w[:], src)
                for oc in range(8):
                    col = (blk * 8 + oc) * 2
                    for kc in range(8):
                        P.mm(ps[:, col:col + 2], w[:, kc, oc * 128:(oc + 1) * 128], condb[:, kc, :],
                             start=(kc == 0), stop=(kc == 7))
            P.tt("dve", C.mod[:, i], ps[:, 0:96].rearrange("p (k j) -> p k j", j=2),
                 C.adab[:, i, :].unsqueeze(2).to_broadcast([128, 48, 2]), ALU.add)
            for (gm, ng, base) in ((C.gm1, C.n1g, 8), (C.gm2, C.n2g, 32)):
                P.ts("dve", gm[:, i], C.mod[:, i, base:base + 8, :], 1.0, None, ALU.add)
                P.tt("dve", gm[:, i], gm[:, i], ng[:, i, :].unsqueeze(2).to_broadcast([128, 8, 2]), ALU.mult)
    return C


def norm_mod(P, C, XT, U, gm, sh):
    with P.scope():
        xts = [P.sb([128, 8, 512], F32) for _ in range(2)]
        sqs = [P.sb([128, 8, 512], BF16) for _ in range(2)]
        rs = [P.sb([128, 512], F32) for _ in range(2)]
        tmp = [P.sb([128, 512], F32) for _ in range(4)]
        nt = 0
        for ti, (t0, n) in enumerate(TILES):
            j = 1 if t0 < TC else 0
            xt, sq, r = xts[ti % 2], sqs[ti % 2], rs[ti % 2]
            P.dma("sp", xt[:, :, :n], fm(XT)[:, :, t0:t0 + n])
            P.actv(sq[:, :, :n], xt[:, :, :n], AF.Square)
            ps = P.bank()
            for c in range(8):
                P.mm(ps[:, :n], C.ones[:], sq[:, c, :n], start=(c == 0), stop=(c == 7))
            P.actv(r[:, :n], ps[:, :n], AF.Sqrt, bias=C.epsb[:, 0:1], scale=1.0 / D)
            P.recip(r[:, :n], r[:, :n])
            for c in range(8):
                tm = tmp[nt % 4]
                nt += 1
                P.stt("dve", tm[:, :n], xt[:, c, :n], gm[:, c, j:j + 1], r[:, :n], ALU.mult, ALU.mult)
                P.actv(U[:, c, pc(t0):pc(t0) + n], tm[:, :n], AF.Identity, bias=sh[:, c, j:j + 1], scale=1.0)


def load_w(P, dst, src2d, eng="pool"):
    N = src2d.shape[1]
    v = src2d.rearrange("(c p) n -> p c n", p=128)
    for n0 in range(0, N, 2048):
        n1 = min(N, n0 + 2048)
        P.dma(eng, dst[:, :, n0:n1], v[:, :, n0:n1])


def ffn_block(P, C, XT, G, i):
    MT = G["MT"]
    with P.scope():
        U = P.sb([128, 8, TP], BF16, name="Uf%d" % i)
        P.memset("pool", U[:, :, 0:1], 0.0)
        P.memset("pool", U[:, :, TC + 1:TC + 2], 0.0)
        P.memset("pool", U[:, :, TP - 1:TP], 0.0)
        norm_mod(P, C, XT, U, C.gm2[:, i], C.mod[:, i, 24:32, :])
        cw = P.sb([128, 44, 4], F32)
        P.dma("sp", cw[:], G["ffn_cw"][i])
        wb = [P.sb([128, 8, 256], BF16) for _ in range(2)]
        hb = [P.sb([128, TP], F32) for _ in range(2)]
        acc = [P.sb([128, TP], F32) for _ in range(2)]
        mb = [P.sb([128, TP], BF16) for _ in range(2)]
        for a in acc:
            P.memset("pool", a[:], 0.0)
        wup = G["ffn_w_up"][i]

        def loadw(c):
            w = wb[c % 2]
            for hlf, cc in enumerate((c, c + NPAIR)):
                P.dma("pool", w[:, :, hlf * 128:(hlf + 1) * 128],
                      wup[:, cc * 128:(cc + 1) * 128].rearrange("(k p) n -> p k n", p=128))
        loadw(0)
        for c in range(NPAIR):
            if c + 1 < NPAIR:
                loadw(c + 1)
            w = wb[c % 2]
            for hlf, cc in enumerate((c, c + NPAIR)):
                for c0 in range(0, TP, 512):
                    n = min(512, TP - c0)
                    ps = P.bank()
                    for k in range(8):
                        P.mm(ps[:, :n], w[:, k, hlf * 128:(hlf + 1) * 128], U[:, k, c0:c0 + n],
                             start=(k == 0), stop=(k == 7))
                    P.copy("act", hb[hlf][:, c0:c0 + n], ps[:, :n])
                L = TP - 2
                P.ts("dve", acc[hlf][:, 1:1 + L], hb[hlf][:, 0:L], cw[:, cc, 0:1], cw[:, cc, 3:4], ALU.mult, ALU.add)
                P.stt("dve", acc[hlf][:, 1:1 + L], hb[hlf][:, 1:1 + L], cw[:, cc, 1:2], acc[hlf][:, 1:1 + L], ALU.mult, ALU.add)
                P.stt("dve", acc[hlf][:, 1:1 + L], hb[hlf][:, 2:2 + L], cw[:, cc, 2:3], acc[hlf][:, 1:1 + L], ALU.mult, ALU.add)
            P.actv(acc[1][:], acc[1][:], AF.Silu)
            m = mb[c % 2]
            P.tt("dve", m[:], acc[0][:], acc[1][:], ALU.mult)
            P.dma("sp", MT[c * 128:(c + 1) * 128, :], m[:])
    with P.scope():
        wd = P.sb([128, NPAIR, D], BF16)
        load_w(P, wd, G["ffn_w_down"][i])
        mts = [P.sb([128, NPAIR, 512], BF16) for _ in range(2)]
        xts = [P.sb([128, 8, 512], F32) for _ in range(2)]
        for ti, (t0, n) in enumerate(TILES):
            j = 1 if t0 < TC else 0
            mt, xt = mts[ti % 2], xts[ti % 2]
            P.dma("sp", mt[:, :, :n], MT.rearrange("(k p) t -> p k t", p=128)[:, :, pc(t0):pc(t0) + n])
            P.dma("sp", xt[:, :, :n], fm(XT)[:, :, t0:t0 + n])
            for oc in range(8):
                ps = P.bank()
                for k in range(NPAIR):
                    P.mm(ps[:, :n], wd[:, k, oc * 128:(oc + 1) * 128], mt[:, k, :n], start=(k == 0), stop=(k == NPAIR - 1))
                P.stt("dve", xt[:, oc, :n], ps[:, :n], C.mod[:, i, 40 + oc, j:j + 1], xt[:, oc, :n], ALU.mult, ALU.add)
            P.dma("sp", fm(XT)[:, :, t0:t0 + n], xt[:, :, :n])


def outproj_res(P, C, XT, OT, W2d, i, kch=8):
    with P.scope():
        wo = P.sb([128, kch, D], BF16)
        load_w(P, wo, W2d)
        ots = [P.sb([128, kch, 512], BF16) for _ in range(2)]
        xts = [P.sb([128, 8, 512], F32) for _ in range(2)]
        for ti, (t0, n) in enumerate(TILES):
            j = 1 if t0 < TC else 0
            ot, xt = ots[ti % 2], xts[ti % 2]
            P.dma("sp", ot[:, :, :n], OT.rearrange("(k p) t -> p k t", p=128)[:, :, t0:t0 + n])
            P.dma("sp", xt[:, :, :n], fm(XT)[:, :, t0:t0 + n])
            for oc in range(8):
                ps = P.bank()
                for k in range(kch):
                    P.mm(ps[:, :n], wo[:, k, oc * 128:(oc + 1) * 128], ot[:, k, :n], start=(k == 0), stop=(k == kch - 1))
                P.stt("dve", xt[:, oc, :n], ps[:, :n], C.mod[:, i, 16 + oc, j:j + 1], xt[:, oc, :n], ALU.mult, ALU.add)
            P.dma("sp", fm(XT)[:, :, t0:t0 + n], xt[:, :, :n])


def final_norm(P, C, XT, OUT):
    with P.scope():
        xts = [P.sb([128, 8, 512], F32) for _ in range(2)]
        sqs = [P.sb([128, 8, 512], BF16) for _ in range(2)]
        rs = [P.sb([128, 512], F32) for _ in range(2)]
        toks = []
        for ti, (t0, n) in enumerate(TILES[1:]):
            xt, sq, r = xts[ti % 2], sqs[ti % 2], rs[ti % 2]
            P.dma("sp", xt[:], fm(XT)[:, :, t0:t0 + n])
            P.actv(sq[:], xt[:], AF.Square)
            ps = P.bank()
            for c in range(8):
                P.mm(ps[:], C.ones[:], sq[:, c, :], start=(c == 0), stop=(c == 7))
            P.actv(r[:], ps[:], AF.Sqrt, bias=C.epsb[:, 0:1], scale=1.0 / D)
            P.recip(r[:], r[:])
            for c in range(8):
                P.stt("dve", xt[:, c, :], xt[:, c, :], C.fing[:, c:c + 1], r[:], ALU.mult, ALU.mult)
            toks.append(P.dma("sp", fm(OUT)[:, :, t0 - TC:t0 - TC + n], xt[:]))
        return toks


IN_SPECS = {
    "xT_in": ([D, T], F32),
    "cond2": ([128, 8, 2], F32),
    "ada_w": ([4, D, 6 * D], F32),
    "ada_b": ([128, 4, 48], F32),
    "norm1_g": ([128, 4, 8], F32),
    "norm2_g": ([128, 4, 8], F32),
    "final_g": ([128, 8], F32),
    "ffn_w_up": ([4, D, 2 * FFN], F32),
    "ffn_cw": ([4, 128, 44, 4], F32),
    "ffn_w_down": ([4, FFN, D], F32),
}


def build(stages, out_full=False, extra_inputs=()):
    P = Prog()
    G = {}
    for k, (shp, dt) in IN_SPECS.items():
        G[k] = P.dram(k, shp, dt, kind="ExternalInput").ap()
    for k, (shp, dt) in MIXER_IN_SPECS.items():
        G[k] = P.dram(k, shp, dt, kind="ExternalInput").ap()
    G["layers"] = sorted(set(i for (_, i) in stages if i is not None))
    XT = P.dram("XT", [D, T], F32).ap()
    G["MT"] = P.dram("MT", [FFN, TP], BF16).ap()
    G["OT"] = P.dram("OT", [D, T], BF16).ap()
    if out_full:
        OUT = P.dram("out", [D, T], F32, kind="ExternalOutput").ap()
    else:
        OUT = P.dram("out", [D, TX], F32, kind="ExternalOutput").ap()
    P.dma("sp", XT[:, :], G["xT_in"][:, :])
    C = setup_common(P, G)
    toks = None
    for (kind, i) in stages:
        if kind == "ffn":
            ffn_block(P, C, XT, G, i)
        elif kind == "mixer":
            MIXERS[i % 4](P, C, XT, G, i)
        elif kind == "final":
            toks = final_norm(P, C, XT, OUT)
    if out_full:
        P.barrier()
        toks = [P.dma("sp", OUT[:, :], XT[:, :])]
    P.finish(toks)
    print("program: instr=%d waits=%d" % (P.ninstr, P.nwaits))
    return P.nc


MIXER_IN_SPECS = {}
MIXERS = {}


def host_common(inp, b):
    f = np.float32
    m = {}
    m["xT_in"] = np.ascontiguousarray(np.concatenate([inp["ctx"][b], inp["x"][b]], axis=0).T.astype(f))
    cond2 = np.stack([inp["c"][b], inp["c_ctx"]], axis=-1)
    m["cond2"] = np.ascontiguousarray(cond2.reshape(8, 128, 2).transpose(1, 0, 2))
    return m


def host_shared(inp):
    f = np.float32
    m = {}
    m["ada_w"] = np.ascontiguousarray(inp["ada_w"], dtype=f)
    m["ada_b"] = np.ascontiguousarray(inp["ada_b"].reshape(4, 48, 128).transpose(2, 0, 1), dtype=f)
    m["norm1_g"] = np.ascontiguousarray(inp["norm1_g"].reshape(4, 8, 128).transpose(2, 0, 1), dtype=f)
    m["norm2_g"] = np.ascontiguousarray(inp["norm2_g"].reshape(4, 8, 128).transpose(2, 0, 1), dtype=f)
    m["final_g"] = np.ascontiguousarray(inp["final_g"].reshape(8, 128).T, dtype=f)
    m["ffn_w_up"] = np.ascontiguousarray(inp["ffn_w_up"], dtype=f)
    cw = np.concatenate([inp["ffn_conv_w"], inp["ffn_conv_b"][:, None, :]], axis=1)
    m["ffn_cw"] = np.ascontiguousarray(cw.reshape(4, 4, 44, 128).transpose(0, 3, 2, 1), dtype=f)
    m["ffn_w_down"] = np.ascontiguousarray(inp["ffn_w_down"], dtype=f)
    return m


def host_shared_all(inp):
    m = host_shared(inp)
    for fn in HOST_MIXERS:
        m.update(fn(inp))
    return m


HOST_MIXERS = []


MLA_SCALE = 96 ** -0.5
MIXER_IN_SPECS.update({
    "mla_w_in": ([D, 672], F32),
    "mla_w_in_krp": ([D, 32], F32),
    "mla_qg": ([128, 3], F32),
    "mla_kvg": ([128, 2], F32),
    "mla_w_q_up": ([384, 1536], F32),
    "mla_w_q_rp": ([384, 512], F32),
    "mla_w_kv": ([256, 2048], F32),
    "mla_w_out": ([D, D], F32),
    "rope_tab": ([32, 2, T], F32),
})


def host_mla(inp):
    f = np.float32
    m = {}
    w_in = inp["mla_w_in"][0]
    perm = np.array([j + 8 if (j % 16) < 8 else j - 8 for j in range(32)])
    m["mla_w_in"] = np.ascontiguousarray(w_in, dtype=f)
    m["mla_w_in_krp"] = np.ascontiguousarray(w_in[:, 640 + perm], dtype=f)
    m["mla_qg"] = np.ascontiguousarray(inp["mla_q_norm_g"][0].reshape(3, 128).T, dtype=f)
    m["mla_kvg"] = np.ascontiguousarray(inp["mla_kv_norm_g"][0].reshape(2, 128).T, dtype=f)
    wq = inp["mla_w_q_up"][0]
    m["mla_w_q_up"] = np.ascontiguousarray(wq, dtype=f)
    cols = np.concatenate([h * 96 + 64 + perm for h in range(16)])
    m["mla_w_q_rp"] = np.ascontiguousarray(wq[:, cols], dtype=f)
    wkv = inp["mla_w_kv_up"][0].reshape(256, 16, 128)
    m["mla_w_kv"] = np.ascontiguousarray(np.concatenate([wkv[:, :, :64].reshape(256, 1024), wkv[:, :, 64:].reshape(256, 1024)], axis=1), dtype=f)
    m["mla_w_out"] = np.ascontiguousarray(inp["mla_w_out"][0], dtype=f)
    tab = np.zeros((32, 2, T), f)
    tab[:, 0, :TC] = 1.0
    t = np.arange(TX)
    inv = (1.0 / (10000.0 ** (np.arange(8, dtype=f) / f(8)))).astype(f)
    for j in range(32):
        pos = (t // 64) if j < 16 else (t % 64)
        ang = pos.astype(f) * inv[j % 8]
        tab[j, 0, TC:] = np.cos(ang).astype(f)
        sn = np.sin(ang).astype(f)
        tab[j, 1, TC:] = -sn if (j % 16) < 8 else sn
    m["rope_tab"] = tab
    return m


HOST_MIXERS.append(host_mla)


def attn_dense(P, Kh, Qh, Vh, OTd, h, qtiles, nkt_of, ots, rdens, pts, state):
    dk = state["dk"]
    for (t0, n) in qtiles:
        nkt = nkt_of(t0)
        ops = state["acc"][state["no"] % 2]
        for kt in range(nkt):
            sps = P.bank()
            P.mm(sps[:, :n], Kh[0:dk, kt * 128:(kt + 1) * 128], Qh[0:dk, t0:t0 + n])
            pt = pts[state["np"] % len(pts)]
            state["np"] += 1
            P.actv(pt[:, :n], sps[:, :n], AF.Exp)
            P.mm(ops[:, :n], Vh[:, kt, :], pt[:, :n], start=(kt == 0), stop=(kt == nkt - 1))
        rd = rdens[state["no"] % 2]
        ot = ots[state["no"] % 2]
        state["no"] += 1
        P.recip(rd[64:128, :n], ops[64:128, :n])
        P.tt("dve", ot[0:64, :n], ops[0:64, :n], rd[64:128, :n], ALU.mult)
        P.dma("sp", OTd[h * 64:(h + 1) * 64, t0:t0 + n], ot[0:64, :n])


def mla_mixer(P, C, XT, G, i):
    nc = P.nc
    QT = P.dram("mla_QT", [16, 96, T], BF16).ap()
    KT = P.dram("mla_KT", [16, 64, T], BF16).ap()
    KR = P.dram("mla_KR", [32, T], BF16).ap()
    VT = P.dram("mla_VT", [16, T, 64], BF16).ap()
    OT = G["OT"]
    with P.scope():
        U = P.sb([128, 8, TP], BF16, name="Um%d" % i)
        norm_mod(P, C, XT, U, C.gm1[:, i], C.mod[:, i, 0:8, :])
        w_in = P.sb([128, 8, 672], BF16)
        w_krp = P.sb([128, 8, 32], BF16)
        wq = P.sb([128, 3, 1536], BF16)
        wqp = P.sb([128, 3, 512], BF16)
        wkv = P.sb([128, 2, 2048], BF16)
        load_w(P, w_in, G["mla_w_in"])
        load_w(P, w_krp, G["mla_w_in_krp"])
        load_w(P, wq, G["mla_w_q_up"])
        load_w(P, wqp, G["mla_w_q_rp"])
        load_w(P, wkv, G["mla_w_kv"])
        qg = P.sb([128, 3], F32)
        kvg = P.sb([128, 2], F32)
        P.dma("sp", qg[:], G["mla_qg"])
        P.dma("sp", kvg[:], G["mla_kvg"])
        tabs = [P.sb([96, 2, 512], F32) for _ in range(2)]
        sq = P.sb([128, 3, 512], BF16)
        rstd = P.sb([128, 512], F32)
        qn = P.sb([128, 3, 512], BF16)
        kvn = P.sb([128, 2, 512], BF16)
        t1 = [P.sb([96, 512], F32) for _ in range(2)]
        t2 = [P.sb([96, 512], F32) for _ in range(2)]
        krot = P.sb([32, 512], BF16)
        Qt = [P.sb([96, 16, 512], BF16) for _ in range(2)]
        Kt = [P.sb([128, 8, 512], BF16) for _ in range(2)]
        Vt = [P.sb([128, 1024], BF16) for _ in range(2)]
        nv = 0
        for ti, (t0, n) in enumerate(TILES):
            p0 = pc(t0)
            tab = tabs[ti % 2]
            P.dma("sp", tab[0:32, :, :n], G["rope_tab"][:, :, t0:t0 + n])
            P.dma("sp", tab[64:96, :, :n], G["rope_tab"][:, :, t0:t0 + n])
            for (nch, col0, g, dst, dim) in ((3, 0, qg, qn, 384), (2, 384, kvg, kvn, 256)):
                lat = [P.bank() for _ in range(nch)]
                for c in range(nch):
                    for k in range(8):
                        P.mm(lat[c][:, :n], w_in[:, k, col0 + c * 128:col0 + (c + 1) * 128], U[:, k, p0:p0 + n],
                             start=(k == 0), stop=(k == 7))
                    P.actv(sq[:, c, :n], lat[c][:, :n], AF.Square)
                ss = P.bank()
                for c in range(nch):
                    P.mm(ss[:, :n], C.ones[:], sq[:, c, :n], start=(c == 0), stop=(c == nch - 1))
                P.actv(rstd[:, :n], ss[:, :n], AF.Sqrt, bias=C.epsb[:, 0:1], scale=1.0 / dim)
                P.recip(rstd[:, :n], rstd[:, :n])
                for c in range(nch):
                    P.stt("dve", dst[:, c, :n], lat[c][:, :n], g[:, c:c + 1], rstd[:, :n], ALU.mult, ALU.mult)
            kr = P.bank()
            krp = P.bank()
            for k in range(8):
                P.mm(kr[0:32, :n], w_in[:, k, 640:672], U[:, k, p0:p0 + n], start=(k == 0), stop=(k == 7))
            for k in range(8):
                P.mm(krp[0:32, :n], w_krp[:, k, :], U[:, k, p0:p0 + n], start=(k == 0), stop=(k == 7))
            P.tt("dve", t1[0][0:32, :n], kr[0:32, :n], tab[0:32, 0, :n], ALU.mult)
            P.tt("dve", t2[0][0:32, :n], krp[0:32, :n], tab[0:32, 1, :n], ALU.mult)
            P.tt("dve", krot[:, :n], t1[0][0:32, :n], t2[0][0:32, :n], ALU.add)
            P.dma("sp", KR[:, t0:t0 + n], krot[:, :n])
            qt = Qt[ti % 2]
            for h in range(16):
                qp = P.bank()
                qr = P.bank()
                for k in range(3):
                    P.mm(qp[0:96, :n], wq[:, k, h * 96:(h + 1) * 96], qn[:, k, :n], start=(k == 0), stop=(k == 2))
                for k in range(3):
                    P.mm(qr[0:32, :n], wqp[:, k, h * 32:(h + 1) * 32], qn[:, k, :n], start=(k == 0), stop=(k == 2))
                P.actv(qt[0:64, h, :n], qp[0:64, :n], AF.Copy, scale=MLA_SCALE)
                a, b = t1[h % 2], t2[h % 2]
                P.tt("dve", a[64:96, :n], qp[64:96, :n], tab[64:96, 0, :n], ALU.mult)
                P.stt("dve", b[64:96, :n], qr[0:32, :n], MLA_SCALE, tab[0:32, 1, :n], ALU.mult, ALU.mult)
                P.stt("dve", qt[64:96, h, :n], a[64:96, :n], MLA_SCALE, b[64:96, :n], ALU.mult, ALU.add)
            P.dma("sp", QT[:, :, t0:t0 + n].rearrange("h r t -> r h t"), qt[:, :, :n])
            kt_ = Kt[ti % 2]
            for hp in range(8):
                kp = P.bank()
                for k in range(2):
                    P.mm(kp[:, :n], wkv[:, k, hp * 128:(hp + 1) * 128], kvn[:, k, :n], start=(k == 0), stop=(k == 1))
                P.copy("act", kt_[:, hp, :n], kp[:, :n])
            P.dma("sp", KT.rearrange("(hp two) r t -> (two r) hp t", two=2)[:, :, t0:t0 + n], kt_[:, :, :n])
            for s in range(n // 128):
                vt = Vt[nv % 2]
                nv += 1
                for hf in range(2):
                    vp = P.bank()
                    for k in range(2):
                        P.mm(vp[:, :], kvn[:, k, s * 128:(s + 1) * 128], wkv[:, k, 1024 + hf * 512:1024 + (hf + 1) * 512],
                             start=(k == 0), stop=(k == 1))
                    P.copy("act", vt[:, hf * 512:(hf + 1) * 512], vp[:, :])
                tt0 = t0 + s * 128
                P.dma("sp", VT[:, tt0:tt0 + 128, :].rearrange("h t d -> t h d"), vt[:].rearrange("p (h d) -> p h d", d=64))
    with P.scope():
        Khs = [P.sb([96, T], BF16) for _ in range(2)]
        Qhs = [P.sb([96, T], BF16) for _ in range(2)]
        Vhs = [P.sb([128, 34, 128], BF16) for _ in range(2)]
        for v in Vhs:
            P.memset("pool", v[:, :, 64:128], 1.0)
        pts = [P.sb([128, 512], BF16) for _ in range(4)]
        ots = [P.sb([64, 512], BF16) for _ in range(2)]
        rdens = [P.sb([128, 512], F32) for _ in range(2)]
        state = {"np": 0, "no": 0, "dk": 96, "acc": P.reserve(2)}

        def loadh(h):
            Kh, Qh, Vh = Khs[h % 2], Qhs[h % 2], Vhs[h % 2]
            P.dma("sp", Kh[0:64, :], KT[h])
            P.dma("sp", Kh[64:96, :], KR[:, :])
            P.dma("sp", Qh[:, :], QT[h])
            P.dma("sp", Vh[:, :, 0:64], VT[h].rearrange("(kt p) d -> p kt d", p=128))
        loadh(0)
        for h in range(16):
            if h + 1 < 16:
                loadh(h + 1)
            attn_dense(P, Khs[h % 2], Qhs[h % 2], Vhs[h % 2], OT, h, TILES, lambda t0: 2 if t0 < TC else 34,
                       ots, rdens, pts, state)
        P.unreserve()
    outproj_res(P, C, XT, OT, G["mla_w_out"], i)


MIXERS[0] = mla_mixer


NA_SCALE = 64 ** -0.5
MIXER_IN_SPECS.update({
    "na_w_qkv": ([D, 3 * D], F32),
    "na_w_out": ([D, D], F32),
    "na_rpb_g": ([64, 16, 15, 64], F32),
    "na_colmask": ([64, 64], F32),
})


def host_natten(inp):
    f = np.float32
    m = {}
    m["na_w_qkv"] = np.ascontiguousarray(inp["na_w_qkv"][0], dtype=f)
    m["na_w_out"] = np.ascontiguousarray(inp["na_w_out"][0], dtype=f)
    rpb = inp["na_rpb"][0]
    wk = np.arange(64)[:, None]
    wq = np.arange(64)[None, :]
    dc = wk - wq + 15
    ok = (dc >= 0) & (dc <= 30)
    dcc = np.clip(dc, 0, 30)
    g = rpb[:, ::-1, :][:, :, dcc]
    g = np.where(ok[None, None], g, f(0))
    m["na_rpb_g"] = np.ascontiguousarray(g.transpose(2, 0, 1, 3), dtype=f)
    c0 = np.clip(np.arange(64) - 8, 0, 48)[None, :]
    m["na_colmask"] = ((wk >= c0) & (wk <= c0 + 15)).astype(f)
    return m


HOST_MIXERS.append(host_natten)


def natten_mixer(P, C, XT, G, i):
    QT = P.dram("na_QT", [16, 64, T], BF16).ap()
    KT = P.dram("na_KT", [16, 64, T], BF16).ap()
    VT = P.dram("na_VT", [16, T, 64], BF16).ap()
    OT = G["OT"]
    with P.scope():
        U = P.sb([128, 8, TP], BF16, name="Un%d" % i)
        norm_mod(P, C, XT, U, C.gm1[:, i], C.mod[:, i, 0:8, :])
        w = P.sb([128, 8, 3 * D], BF16)
        load_w(P, w, G["na_w_qkv"])
        Qt = [P.sb([128, 8, 512], BF16) for _ in range(2)]
        Kt = [P.sb([128, 8, 512], BF16) for _ in range(2)]
        Vt = [P.sb([128, 1024], BF16) for _ in range(2)]
        nv = 0
        for ti, (t0, n) in enumerate(TILES):
            p0 = pc(t0)
            for (dst, DT, cbase, scl) in ((Qt[ti % 2], QT, 0, NA_SCALE), (Kt[ti % 2], KT, D, 1.0)):
                for hp in range(8):
                    ps = P.bank()
                    for k in range(8):
                        P.mm(ps[:, :n], w[:, k, cbase + hp * 128:cbase + (hp + 1) * 128], U[:, k, p0:p0 + n],
                             start=(k == 0), stop=(k == 7))
                    P.actv(dst[:, hp, :n], ps[:, :n], AF.Copy, scale=scl)
                P.dma("sp", DT.rearrange("(hp two) r t -> (two r) hp t", two=2)[:, :, t0:t0 + n], dst[:, :, :n])
            for s in range(n // 128):
                vt = Vt[nv % 2]
                nv += 1
                for hf in range(2):
                    vp = P.bank()
                    for k in range(8):
                        P.mm(vp[:, :], U[:, k, p0 + s * 128:p0 + (s + 1) * 128],
                             w[:, k, 2 * D + hf * 512:2 * D + (hf + 1) * 512], start=(k == 0), stop=(k == 7))
                    P.copy("act", vt[:, hf * 512:(hf + 1) * 512], vp[:, :])
                tt0 = t0 + s * 128
                P.dma("sp", VT[:, tt0:tt0 + 128, :].rearrange("h t d -> t h d"), vt[:].rearrange("p (h d) -> p h d", d=64))
    with P.scope():
        GR = P.sb([64, 16, 15 * 64], BF16)
        cm = P.sb([64, 64], F32)
        P.dma("sp", cm[:], G["na_colmask"])
        rp = [P.sb([64, 15, 64], F32) for _ in range(2)]
        for h in range(16):
            r = rp[h % 2]
            P.dma("sp", r[:], G["na_rpb_g"][:, h])
            P.actv(r[:], r[:], AF.Exp)
            P.tt("dve", GR[:, h, :].rearrange("p (k w) -> p k w", w=64), r[:],
                 cm[:].unsqueeze(1).to_broadcast([64, 15, 64]), ALU.mult)
        Khs = [P.sb([64, T], BF16) for _ in range(2)]
        Qhs = [P.sb([64, T], BF16) for _ in range(2)]
        Vcs = [P.sb([128, 2, 128], BF16) for _ in range(2)]
        Vgs = [P.sb([64, 64, 128], BF16) for _ in range(2)]
        for v in Vcs + Vgs:
            P.memset("pool", v[:, :, 64:128], 1.0)
        pts = [P.sb([128, 512], BF16) for _ in range(4)]
        es = [P.sb([64, 512], F32) for _ in range(3)]
        ots = [P.sb([64, 512], BF16) for _ in range(2)]
        rdens = [P.sb([128, 512], F32) for _ in range(2)]
        state = {"np": 0, "no": 0, "dk": 64, "acc": P.reserve(2)}
        ne = 0

        def loadh(h):
            P.dma("sp", Khs[h % 2][:, :], KT[h])
            P.dma("sp", Qhs[h % 2][:, :], QT[h])
            P.dma("sp", Vcs[h % 2][:, :, 0:64], VT[h, 0:TC, :].rearrange("(kt p) d -> p kt d", p=128))
            P.dma("sp", Vgs[h % 2][:, :, 0:64], VT[h, TC:T, :].rearrange("(r w) d -> w r d", w=64))
        loadh(0)
        for h in range(16):
            if h + 1 < 16:
                loadh(h + 1)
            Kh, Qh, Vc, Vg = Khs[h % 2], Qhs[h % 2], Vcs[h % 2], Vgs[h % 2]
            attn_dense(P, Kh, Qh, Vc, OT, h, [(0, TC)], lambda t0: 2, ots, rdens, pts, state)
            for qb in range(8):
                t0 = TC + 512 * qb
                ops = state["acc"][state["no"] % 2]
                for kt in range(2):
                    sps = P.bank()
                    P.mm(sps[:, :], Kh[:, kt * 128:(kt + 1) * 128], Qh[:, t0:t0 + 512])
                    pt = pts[state["np"] % 4]
                    state["np"] += 1
                    P.actv(pt[:, :], sps[:, :], AF.Exp)
                    P.mm(ops[:, :], Vc[:, kt, :], pt[:, :], start=(kt == 0), stop=False)
                rows = []
                for rk in range(64):
                    val = [r for r in range(8 * qb, 8 * qb + 8) if min(max(r - 4, 0), 56) <= rk <= min(max(r - 4, 0), 56) + 7]
                    if val:
                        rows.append((rk, val[0], len(val)))
                for idx, (rk, ra, nj) in enumerate(rows):
                    q0 = (ra - 8 * qb) * 64
                    nq = nj * 64
                    ka = 7 - rk + ra
                    sps = P.bank()
                    P.mm(sps[0:64, :nq], Kh[:, TC + rk * 64:TC + (rk + 1) * 64], Qh[:, t0 + q0:t0 + q0 + nq])
                    e = es[ne % 3]
                    ne += 1
                    P.actv(e[:, :nq], sps[0:64, :nq], AF.Exp)
                    pt = pts[state["np"] % 4]
                    state["np"] += 1
                    P.tt("dve", pt[0:64, :nq], e[:, :nq], GR[:, h, ka * 64:ka * 64 + nq], ALU.mult)
                    P.mm(ops[:, q0:q0 + nq], Vg[:, rk, :], pt[0:64, :nq], start=False, stop=(idx == len(rows) - 1))
                rd = rdens[state["no"] % 2]
                ot = ots[state["no"] % 2]
                state["no"] += 1
                P.recip(rd[64:128, :], ops[64:128, :])
                P.tt("dve", ot[0:64, :], ops[0:64, :], rd[64:128, :], ALU.mult)
                P.dma("sp", OT[h * 64:(h + 1) * 64, t0:t0 + 512], ot[0:64, :])
        P.unreserve()
    outproj_res(P, C, XT, OT, G["na_w_out"], i)


MIXERS[3] = natten_mixer


MIXER_IN_SPECS.update({
    "s5_w_in": ([D, 512], F32),
    "s5_par": ([128, 3, 32], F32),
    "s5_B": ([128, 2, 2, 16, 128], F32),
    "s5_C": ([128, 2, 2, 16, 32], F32),
    "s5_dsk": ([128, 4], F32),
    "s5_w_glu": ([512, 2 * D], F32),
})


def host_s5(inp):
    f = np.float32
    m = {}
    m["s5_w_in"] = np.ascontiguousarray(inp["s5_w_in"][0], dtype=f)
    lre, lim, ldt = inp["s5_lambda_re"][0], inp["s5_lambda_im"][0], inp["s5_log_dt"][0]
    par = np.zeros((128, 3, 2, 16), f)
    B = np.zeros((128, 2, 2, 16, 128), f)
    Cm = np.zeros((128, 2, 2, 16, 32), f)
    bre, bim = inp["s5_b_re"][0], inp["s5_b_im"][0]
    cre, cim = inp["s5_c_re"][0], inp["s5_c_im"][0]
    for g in range(32):
        s, gh = g // 2, g % 2
        for d in range(2):
            par[gh * 64:(gh + 1) * 64, 0, d, s] = lre[d, g]
            par[gh * 64:(gh + 1) * 64, 1, d, s] = lim[d, g]
            par[gh * 64:(gh + 1) * 64, 2, d, s] = ldt[d, g]
            pp = (s % 4) * 32 + gh * 16
            B[pp:pp + 16, 0, d, s, gh * 64:(gh + 1) * 64] = bre[d, g].T
            B[pp:pp + 16, 1, d, s, gh * 64:(gh + 1) * 64] = bim[d, g].T
            Cm[gh * 64:(gh + 1) * 64, 0, d, s, gh * 16:(gh + 1) * 16] = cre[d, g].T
            Cm[gh * 64:(gh + 1) * 64, 1, d, s, gh * 16:(gh + 1) * 16] = cim[d, g].T
    m["s5_par"] = par.reshape(128, 3, 32)
    m["s5_B"] = B
    m["s5_C"] = Cm
    m["s5_dsk"] = np.ascontiguousarray(inp["s5_d"][0].reshape(4, 128).T, dtype=f)
    m["s5_w_glu"] = np.ascontiguousarray(inp["s5_w_glu"][0], dtype=f)
    return m


HOST_MIXERS.append(host_s5)


def _horner(P, out, x2, coefs, tmpa):
    cs = list(coefs)
    P.ts("dve", out, x2, cs[-1], cs[-2], ALU.mult, ALU.add)
    for c in reversed(cs[:-2]):
        P.tt("dve", tmpa, out, x2, ALU.mult)
        P.ts("dve", out, tmpa, c, None, ALU.add)


def s5_mixer(P, C, XT, G, i):
    import math
    DBG = 0
    ZT = P.dram("s5_ZT", [512, T], F32).ap()
    YT = P.dram("s5_YT", [512, T], F32).ap()
    I32 = mybir.dt.int32
    with P.scope():
        zb = P.sb([128, 4, T], BF16, name="s5_zb")
        with P.scope():
            U = P.sb([128, 8, TP], BF16, name="Us%d" % i)
            norm_mod(P, C, XT, U, C.gm1[:, i], C.mod[:, i, 0:8, :])
            w = P.sb([128, 8, 512], BF16)
            load_w(P, w, G["s5_w_in"])
            zf = [P.sb([128, 4, 512], F32) for _ in range(2)]
            for ti, (t0, n) in enumerate(TILES):
                p0 = pc(t0)
                z = zf[ti % 2]
                for c in range(4):
                    ps = P.bank()
                    for k in range(8):
                        P.mm(ps[:, :n], w[:, k, c * 128:(c + 1) * 128], U[:, k, p0:p0 + n], start=(k == 0), stop=(k == 7))
                    P.copy("act", z[:, c, :n], ps[:, :n])
                    P.copy("dve", zb[:, c, t0:t0 + n], ps[:, :n])
                P.dma("sp", ZT.rearrange("(c p) t -> p c t", p=128)[:, :, t0:t0 + n], z[:, :, :n])
        par = P.sb([128, 3, 32], F32)
        P.dma("sp", par[:], G["s5_par"])
        Bb = P.sb([128, 2, 2, 16, 128], BF16)
        with P.scope():
            Bf = P.sb([128, 2, 2, 16, 128], F32)
            P.dma("sp", Bf[:], G["s5_B"])
            P.copy("dve", Bb[:], Bf[:])
        Cf = P.sb([128, 2, 2, 16, 32], F32)
        P.dma("sp", Cf[:], G["s5_C"])
        Cc = P.sb([128, 2, 32, 32], BF16)
        sm = [P.sb([128, 32], F32, name="s5sm%d" % j) for j in range(24)]
        ki = P.sb([128, 32], I32)
        lre, lim, ldt = par[:, 0, :], par[:, 1, :], par[:, 2, :]
        dt, xx, mag, ang, kf, r, x, x2, ps_, pc_, ta, s2, sn, cs, are, aim, num_re, num_im, den, cfr, cfi, ncfr, ncfi, tb = [t[:] for t in sm]
        P.actv(dt, ldt, AF.Exp)
        P.tt("dve", xx, lre, dt, ALU.mult)
        _horner(P, mag, xx, [1.0, 1.0, 1 / 2., 1 / 6., 1 / 24., 1 / 120., 1 / 720., 1 / 5040.], ta)
        P.tt("dve", ang, lim, dt, ALU.mult)
        P.ts("dve", kf, ang, 1.0 / (2 * math.pi), None, ALU.mult)
        P.copy("dve", ki[:], kf)
        P.copy("dve", kf, ki[:])
        P.stt("dve", r, kf, -6.28125, ang, ALU.mult, ALU.add)
        P.stt("dve", r, kf, -(2 * math.pi - 6.28125), r, ALU.mult, ALU.add)
        P.ts("dve", x, r, 0.5, None, ALU.mult)
        P.tt("dve", x2, x, x, ALU.mult)
        f_ = math.factorial
        _horner(P, ps_, x2, [1.0, -1. / f_(3), 1. / f_(5), -1. / f_(7), 1. / f_(9), -1. / f_(11), 1. / f_(13)], ta)
        _horner(P, pc_, x2, [1.0, -1. / f_(2), 1. / f_(4), -1. / f_(6), 1. / f_(8), -1. / f_(10), 1. / f_(12), -1. / f_(14)], ta)
        P.tt("dve", s2, ps_, x, ALU.mult)
        P.tt("dve", sn, s2, pc_, ALU.mult)
        P.ts("dve", sn, sn, 2.0, None, ALU.mult)
        P.tt("dve", cs, s2, s2, ALU.mult)
        P.ts("dve", cs, cs, -2.0, 1.0, ALU.mult, ALU.add)
        P.tt("dve", are, mag, cs, ALU.mult)
        P.tt("dve", aim, mag, sn, ALU.mult)
        P.ts("dve", ta, are, -1.0, None, ALU.add)
        P.tt("dve", num_re, ta, lre, ALU.mult)
        P.tt("dve", tb, aim, lim, ALU.mult)
        P.tt("dve", num_re, num_re, tb, ALU.add)
        P.tt("dve", num_im, aim, lre, ALU.mult)
        P.tt("dve", tb, ta, lim, ALU.mult)
        P.tt("dve", num_im, num_im, tb, ALU.subtract)
        P.tt("dve", den, lre, lre, ALU.mult)
        P.tt("dve", tb, lim, lim, ALU.mult)
        P.tt("dve", den, den, tb, ALU.add)
        P.recip(den, den)
        P.tt("dve", cfr, num_re, den, ALU.mult)
        P.tt("dve", cfi, num_im, den, ALU.mult)
        P.ts("dve", ncfr, cfr, -1.0, None, ALU.mult)
        P.ts("dve", ncfi, cfi, -1.0, None, ALU.mult)
        ct = P.sb([128, 32], F32)
        for d in range(2):
            for s in range(16):
                j = d * 16 + s
                P.ts("dve", ct[:], Cf[:, 0, d, s, :], cfr[:, j:j + 1], None, ALU.mult)
                P.stt("dve", Cc[:, 0, j, :], Cf[:, 1, d, s, :], ncfi[:, j:j + 1], ct[:], ALU.mult, ALU.add)
                P.ts("dve", ct[:], Cf[:, 0, d, s, :], ncfi[:, j:j + 1], None, ALU.mult)
                P.stt("dve", Cc[:, 1, j, :], Cf[:, 1, d, s, :], ncfr[:, j:j + 1], ct[:], ALU.mult, ALU.add)
        E = P.sb([128, 2, T], F32, name="s5_E")
        Gb = P.sb([128, 2, T], F32, name="s5_G")
        H = [[P.sb([128, T], BF16) for _ in range(2)] for _ in range(2)]
        pw = P.sb([128, 4], F32)
        tm = [P.sb([128, 512], F32) for _ in range(4)]
        ysb = [P.sb([32, 512], F32) for _ in range(2)]
        ny = 0

        def kidx(t):
            return (TC - 1 - t) if t < TC else (TC + T - 1 - t)

        def ev(c, t0, n, d):
            if d == 0:
                return E[:, c, t0:t0 + n]
            lo, hi = kidx(t0 + n - 1), kidx(t0)
            return E[:, c, lo:hi + 1][:, ::-1]

        for s in range(16 if DBG == 0 else 1):
            for d in range(2):
                j = d * 16 + s
                if DBG == 1:
                    continue
                P.memset("dve", E[:, 0, 0:1], 1.0)
                P.memset("dve", E[:, 1, 0:1], 0.0)
                P.copy("dve", pw[:, 0:1], cs[:, j:j + 1])
                P.copy("dve", pw[:, 1:2], sn[:, j:j + 1])
                mlen = 1
                while mlen < T:
                    L = min(mlen, T - mlen)
                    P.ts("dve", pw[:, 2:3], pw[:, 1:2], -1.0, None, ALU.mult)
                    P.ts("dve", E[:, 0, mlen:mlen + L], E[:, 0, 0:L], pw[:, 0:1], None, ALU.mult)
                    P.stt("dve", E[:, 0, mlen:mlen + L], E[:, 1, 0:L], pw[:, 2:3], E[:, 0, mlen:mlen + L], ALU.mult, ALU.add)
                    P.ts("dve", E[:, 1, mlen:mlen + L], E[:, 0, 0:L], pw[:, 1:2], None, ALU.mult)
                    P.stt("dve", E[:, 1, mlen:mlen + L], E[:, 1, 0:L], pw[:, 0:1], E[:, 1, mlen:mlen + L], ALU.mult, ALU.add)
                    mlen *= 2
                    if mlen < T:
                        P.tt("dve", pw[:, 3:4], pw[:, 1:2], pw[:, 1:2], ALU.mult)
                        P.stt("dve", pw[:, 1:2], pw[:, 1:2], 2.0, pw[:, 0:1], ALU.mult, ALU.mult)
                        P.tt("dve", pw[:, 0:1], pw[:, 0:1], pw[:, 0:1], ALU.mult)
                        P.tt("dve", pw[:, 0:1], pw[:, 0:1], pw[:, 3:4], ALU.subtract)
                if DBG == 2:
                    continue
                pb = (s % 4) * 32
                for (t0, n) in TILES:
                    pre, pim = P.bank(), P.bank()
                    P.mm(pre[:, :n], Bb[:, 0, d, s, :], zb[:, s // 4, t0:t0 + n])
                    P.mm(pim[:, :n], Bb[:, 1, d, s, :], zb[:, s // 4, t0:t0 + n])
                    er, ei = ev(0, t0, n, d), ev(1, t0, n, d)
                    P.tt("dve", tm[0][:, :n], pre[:, :n], er, ALU.mult)
                    P.tt("dve", tm[1][:, :n], pim[:, :n], ei, ALU.mult)
                    P.tt("pool", Gb[:, 0, t0:t0 + n], tm[0][:, :n], tm[1][:, :n], ALU.add)
                    P.tt("dve", tm[2][:, :n], pim[:, :n], er, ALU.mult)
                    P.tt("dve", tm[3][:, :n], pre[:, :n], ei, ALU.mult)
                    P.tt("pool", Gb[:, 1, t0:t0 + n], tm[2][:, :n], tm[3][:, :n], ALU.subtract)
                if DBG == 3:
                    continue
                for c in range(2):
                    if d == 0:
                        P.scan(Gb[:, c, :], mag[:, j:j + 1].to_broadcast([128, T]), Gb[:, c, :], 0.0)
                    else:
                        P.scan(Gb[:, c, 0:TC][:, ::-1], mag[:, j:j + 1].to_broadcast([128, TC]), Gb[:, c, 0:TC][:, ::-1], 0.0)
                        P.scan(Gb[:, c, TC:T][:, ::-1], mag[:, j:j + 1].to_broadcast([128, TX]), Gb[:, c, TC:T][:, ::-1],
                               Gb[:, c, 0:1])
                if DBG == 4:
                    continue
                for (t0, n) in TILES:
                    er, ei = ev(0, t0, n, d), ev(1, t0, n, d)
                    gr, gi = Gb[:, 0, t0:t0 + n], Gb[:, 1, t0:t0 + n]
                    P.tt("dve", tm[0][:, :n], er, gr, ALU.mult)
                    P.tt("dve", tm[1][:, :n], ei, gi, ALU.mult)
                    P.tt("pool", H[d][0][:, t0:t0 + n], tm[0][:, :n], tm[1][:, :n], ALU.subtract)
                    P.tt("dve", tm[2][:, :n], er, gi, ALU.mult)
                    P.tt("dve", tm[3][:, :n], ei, gr, ALU.mult)
                    P.tt("pool", H[d][1][:, t0:t0 + n], tm[2][:, :n], tm[3][:, :n], ALU.add)
            for (t0, n) in (TILES if DBG in (0, 6) else []):
                yp = P.bank()
                P.mm(yp[0:32, :n], Cc[:, 0, s, :], H[0][0][:, t0:t0 + n], start=True, stop=False)
                P.mm(yp[0:32, :n], Cc[:, 1, s, :], H[0][1][:, t0:t0 + n], start=False, stop=False)
                P.mm(yp[0:32, :n], Cc[:, 0, 16 + s, :], H[1][0][:, t0:t0 + n], start=False, stop=False)
                P.mm(yp[0:32, :n], Cc[:, 1, 16 + s, :], H[1][1][:, t0:t0 + n], start=False, stop=True)
                y = ysb[ny % 2]
                ny += 1
                P.copy("act", y[:, :n], yp[0:32, :n])
                P.dma("sp", YT[s * 32:(s + 1) * 32, t0:t0 + n], y[:, :n])
    with P.scope():
        wg = P.sb([128, 4, 2 * D], BF16)
        load_w(P, wg, G["s5_w_glu"])
        dsk = P.sb([128, 4], F32)
        P.dma("sp", dsk[:], G["s5_dsk"])
        ys = [P.sb([128, 4, 512], F32) for _ in range(2)]
        zs = [P.sb([128, 4, 512], F32) for _ in range(2)]
        t1 = P.sb([128, 4, 512], F32)
        yb = [P.sb([128, 4, 512], BF16) for _ in range(2)]
        xts = [P.sb([128, 8, 512], F32) for _ in range(2)]
        sg = [P.sb([128, 512], F32) for _ in range(2)]
        for ti, (t0, n) in enumerate(TILES):
            jx = 1 if t0 < TC else 0
            y, z, xt, ybb = ys[ti % 2], zs[ti % 2], xts[ti % 2], yb[ti % 2]
            P.dma("sp", y[:, :, :n], YT.rearrange("(c p) t -> p c t", p=128)[:, :, t0:t0 + n])
            P.dma("sp", z[:, :, :n], ZT.rearrange("(c p) t -> p c t", p=128)[:, :, t0:t0 + n])
            P.dma("sp", xt[:, :, :n], fm(XT)[:, :, t0:t0 + n])
            for c in range(4):
                P.stt("dve", y[:, c, :n], z[:, c, :n], dsk[:, c:c + 1], y[:, c, :n], ALU.mult, ALU.add)
            P.actv(t1[:, :, :n], y[:, :, :n], AF.Square)
            P.ts("dve", t1[:, :, :n], t1[:, :, :n], 0.044715, 1.0, ALU.mult, ALU.add)
            P.tt("dve", t1[:, :, :n], t1[:, :, :n], y[:, :, :n], ALU.mult)
            P.actv(t1[:, :, :n], t1[:, :, :n], AF.Sigmoid, scale=1.5957691216057308)
            P.tt("dve", ybb[:, :, :n], t1[:, :, :n], y[:, :, :n], ALU.mult)
            for oc in range(8):
                pa, pg = P.bank(), P.bank()
                for k in range(4):
                    P.mm(pa[:, :n], wg[:, k, oc * 128:(oc + 1) * 128], ybb[:, k, :n], start=(k == 0), stop=(k == 3))
                for k in range(4):
                    P.mm(pg[:, :n], wg[:, k, D + oc * 128:D + (oc + 1) * 128], ybb[:, k, :n], start=(k == 0), stop=(k == 3))
                s_ = sg[oc % 2]
                P.actv(s_[:, :n], pg[:, :n], AF.Sigmoid)
                P.tt("dve", s_[:, :n], pa[:, :n], s_[:, :n], ALU.mult)
                P.stt("dve", xt[:, oc, :n], s_[:, :n], C.mod[:, i, 16 + oc, jx:jx + 1], xt[:, oc, :n], ALU.mult, ALU.add)
            P.dma("sp", fm(XT)[:, :, t0:t0 + n], xt[:, :, :n])


MIXERS[1] = s5_mixer


HG_SCALE = 128 ** -0.5
MIXER_IN_SPECS.update({
    "hg_w_in": ([D, 5 * D], F32),
    "hg_w_out": ([D, D], F32),
    "hg_ng": ([128, 8], F32),
    "hg_lbraw": ([128, 8, 4], F32),
})


def host_hg(inp):
    f = np.float32
    m = {}
    m["hg_w_in"] = np.ascontiguousarray(inp["hg_w_in"][0], dtype=f)
    m["hg_w_out"] = np.ascontiguousarray(inp["hg_w_out"][0], dtype=f)
    m["hg_ng"] = np.ascontiguousarray(inp["hg_norm_g"][0].reshape(8, 128).T, dtype=f)
    m["hg_lbraw"] = np.ascontiguousarray(inp["hg_lower_bound"].reshape(4, 8, 128).transpose(2, 1, 0), dtype=f)
    return m


HOST_MIXERS.append(host_hg)


def hgrn2_mixer(P, C, XT, G, i):
    QF = P.dram("hg_QF", [D, T], F32).ap()
    FF = [P.dram("hg_F%d" % d, [D, T], F32).ap() for d in range(2)]
    GS = P.dram("hg_GS", [D, T], F32).ap()
    VH = P.dram("hg_VH", [T, D], BF16).ap()
    OT = G["OT"]
    NCK = T // 64
    HDBG = 0
    with P.scope():
        lbr = P.sb([128, 8, 4], F32)
        P.dma("sp", lbr[:], G["hg_lbraw"])
        P.actv(lbr[:], lbr[:], AF.Exp)
        ssum = P.sb([128, 8], F32)
        lb = P.sb([128, 8], F32)
        oml = P.sb([128, 8], F32)
        ng = P.sb([128, 8], F32)
        P.dma("sp", ng[:], G["hg_ng"])
        P.tt("dve", ssum[:], lbr[:, :, 0], lbr[:, :, 1], ALU.add)
        P.tt("dve", ssum[:], ssum[:], lbr[:, :, 2], ALU.add)
        P.tt("dve", ssum[:], ssum[:], lbr[:, :, 3], ALU.add)
        P.recip(ssum[:], ssum[:])
        P.memset("dve", lb[:], 0.0)
        for l in range(1, i + 1):
            P.tt("dve", lb[:], lb[:], lbr[:, :, l], ALU.add)
        P.tt("dve", lb[:], lb[:], ssum[:], ALU.mult)
        P.ts("dve", oml[:], lb[:], -1.0, 1.0, ALU.mult, ALU.add)
        with P.scope():
            U = P.sb([128, 8, TP], BF16, name="Uh%d" % i)
            norm_mod(P, C, XT, U, C.gm1[:, i], C.mod[:, i, 0:8, :])
            wbs = [P.sb([128, 8, D], BF16) for _ in range(2)]
            st = [P.sb([128, 8, 512], F32) for _ in range(2)]
            sgm = [P.sb([128, 512], F32) for _ in range(2)]
            Vt = [P.sb([128, D], BF16) for _ in range(2)]
            nst = 0
            nv = 0
            for bi, blk in enumerate((0, 1, 2, 4, 3)):
                wb = wbs[bi % 2]
                load_w(P, wb, G["hg_w_in"][:, blk * D:(blk + 1) * D])
                for (t0, n) in TILES:
                    p0 = pc(t0)
                    if blk == 3:
                        for s in range(n // 128):
                            vt = Vt[nv % 2]
                            nv += 1
                            for hf in range(2):
                                vp = P.bank()
                                for k in range(8):
                                    P.mm(vp[:, :], U[:, k, p0 + s * 128:p0 + (s + 1) * 128], wb[:, k, hf * 512:(hf + 1) * 512],
                                         start=(k == 0), stop=(k == 7))
                                P.copy("act", vt[:, hf * 512:(hf + 1) * 512], vp[:, :])
                            tt0 = t0 + s * 128
                            P.dma("sp", VH[tt0:tt0 + 128, :], vt[:])
                        continue
                    stg = st[nst % 2]
                    nst += 1
                    for c in range(8):
                        ps = P.bank()
                        for k in range(8):
                            P.mm(ps[:, :n], wb[:, k, c * 128:(c + 1) * 128], U[:, k, p0:p0 + n], start=(k == 0), stop=(k == 7))
                        if blk == 0:
                            P.actv(stg[:, c, :n], ps[:, :n], AF.Silu)
                        elif blk == 4:
                            sg = sgm[c % 2]
                            P.actv(sg[:, :n], ps[:, :n], AF.Silu)
                            P.ts("dve", stg[:, c, :n], sg[:, :n], ng[:, c:c + 1], None, ALU.mult)
                        else:
                            sg = sgm[c % 2]
                            P.actv(sg[:, :n], ps[:, :n], AF.Sigmoid)
                            P.ts("dve", stg[:, c, :n], sg[:, :n], oml[:, c:c + 1], lb[:, c:c + 1], ALU.mult, ALU.add)
                    dstT = {0: QF, 1: FF[0], 2: FF[1], 4: GS}[blk]
                    P.dma("sp", fm(dstT)[:, :, t0:t0 + n], stg[:, :, :n])
        with P.scope():
            rm = [P.sb([128, T], F32) for _ in range(2)]
            P.memset("pool", rm[0][:], 1.0)
            P.memset("pool", rm[1][:], 1.0)
            P.memset("pool", rm[0][:, 0:T:64], 0.0)
            P.memset("pool", rm[1][:, 63:T:64], 0.0)
            msk = [P.sb([64, 64], F32) for _ in range(2)]
            P.ts("dve", msk[0][:], C.iot[0:64, 0:64], 0.0, None, ALU.is_ge)
            P.ts("dve", msk[1][:], C.iot[0:64, 0:64], 0.0, None, ALU.is_le)
            q = P.sb([128, T], F32)
            A = P.sb([128, T], F32)
            Bq = P.sb([128, T], F32)
            Cb = P.sb([128, T], F32)
            qin = [P.sb([128, T], BF16) for _ in range(2)]
            kin = [P.sb([128, T], BF16) for _ in range(2)]
            kout = [P.sb([128, T], BF16) for _ in range(2)]
            ebl = [P.sb([128, NCK], F32) for _ in range(2)]
            v = P.sb([64, NCK, 128], BF16)
            oacc = P.sb([128, T], F32)
            S = [P.sb([128, 128], F32) for _ in range(2)]
            Sb = [P.sb([128, 128], BF16) for _ in range(2)]
            attm = [P.sb([64, 64], BF16) for _ in range(4)]
            ko = [P.sb([64, 128], BF16) for _ in range(4)]
            sq = [P.sb([128, 512], BF16) for _ in range(2)]
            rs = [P.sb([128, 512], F32) for _ in range(2)]
            gst = [P.sb([128, 512], F32) for _ in range(2)]
            onb = [P.sb([128, 512], BF16) for _ in range(2)]
            order = [list(range(NCK)), [3, 2, 1, 0] + list(range(NCK - 1, 3, -1))]
            na = 0
            for h in range(8 if HDBG != 1 else 0):
                hs = slice(h * 128, (h + 1) * 128)
                P.dma("sp", q[:], QF[hs, :])
                P.dma("sp", v[:], VH[:, hs].rearrange("(c t) d -> t c d", t=64))
                for d in range(2):
                    P.dma("sp", A[:], FF[d][hs, :])
                    P.actv(Bq[:], A[:], AF.Ln)
                    if d == 0:
                        P.scan(Bq[:], rm[0][:], Bq[:], 0.0, ALU.mult, ALU.add)
                        bl = Bq[:, 63:T:64]
                    else:
                        P.scan(Bq[:, ::-1], rm[1][:, ::-1], Bq[:, ::-1], 0.0, ALU.mult, ALU.add)
                        bl = Bq[:, 0:T:64]
                    P.actv(ebl[d][:], bl, AF.Exp)
                    P.actv(Cb[:], Bq[:], AF.Exp)
                    P.stt("dve", qin[d][:], q[:], HG_SCALE, Cb[:], ALU.mult, ALU.mult)
                    P.ts("dve", A[:], A[:], -1.0, 1.0, ALU.mult, ALU.add)
                    P.actv(Cb[:], Bq[:], AF.Exp, scale=-1.0)
                    P.tt("dve", kin[d][:], A[:], Cb[:], ALU.mult)
                    P.tt("dve", Cb[:].rearrange("p (c t) -> p c t", t=64), bl.unsqueeze(2).to_broadcast([128, NCK, 64]),
                         Bq[:].rearrange("p (c t) -> p c t", t=64), ALU.subtract)
                    P.actv(Cb[:], Cb[:], AF.Exp)
                    P.tt("dve", kout[d][:], A[:], Cb[:], ALU.mult)
                    P.memset("pool", S[d][:], 0.0)
                    P.memset("pool", Sb[d][:], 0.0)
                P.memset("pool", oacc[:], 0.0)
                for step in range(NCK if HDBG != 2 else 0):
                    for d in range(2):
                        ci = order[d][step]
                        cs = slice(ci * 64, ci * 64 + 64)
                        am, kk = attm[na % 4], ko[na % 4]
                        na += 1
                        ap_ = P.bank()
                        P.mm(ap_[0:64, 0:64], kin[d][:, cs], qin[d][:, cs])
                        P.tt("dve", am[:], ap_[0:64, 0:64], msk[d][:], ALU.mult)
                        if HDBG == 3:
                            continue
                        kp = P.bank()
                        P.mm(kp[0:64, 0:128], kout[d][:, cs], C.ident[:])
                        P.copy("act", kk[:], kp[0:64, 0:128])
                        if HDBG == 4:
                            continue
                        op_ = P.bank()
                        P.mm(op_[:, 0:64], v[:, ci, :], am[:], start=True, stop=False)
                        P.mm(op_[:, 0:64], Sb[d][:], qin[d][:, cs], start=False, stop=True)
                        P.tt("dve", oacc[:, cs], op_[:, 0:64], oacc[:, cs], ALU.add)
                        if HDBG == 5:
                            continue
                        dp = P.bank()
                        P.mm(dp[:, 0:128], kk[:], v[:, ci, :])
                        P.ts("dve", S[d][:], S[d][:], ebl[d][:, ci:ci + 1], None, ALU.mult)
                        P.tt("dve", S[d][:], dp[:, 0:128], S[d][:], ALU.add)
                        P.copy("act", Sb[d][:], S[d][:])
                for ti, (t0, n) in enumerate(TILES):
                    s_, r_, g_, o_ = sq[ti % 2], rs[ti % 2], gst[ti % 2], onb[ti % 2]
                    P.dma("sp", g_[:, :n], GS[hs, t0:t0 + n])
                    P.actv(s_[:, :n], oacc[:, t0:t0 + n], AF.Square)
                    ps = P.bank()
                    P.mm(ps[:, :n], C.ones[:], s_[:, :n])
                    P.actv(r_[:, :n], ps[:, :n], AF.Sqrt, bias=C.epsb[:, 0:1], scale=1.0 / 128)
                    P.recip(r_[:, :n], r_[:, :n])
                    P.tt("dve", r_[:, :n], r_[:, :n], g_[:, :n], ALU.mult)
                    P.tt("dve", o_[:, :n], oacc[:, t0:t0 + n], r_[:, :n], ALU.mult)
                    P.dma("sp", OT[hs, t0:t0 + n], o_[:, :n])
    outproj_res(P, C, XT, OT, G["hg_w_out"], i)


MIXERS[2] = hgrn2_mixer


_NC_CACHE = {}


def kernel(**inputs):
    inp = {k: np.asarray(v) for k, v in inputs.items()}
    nb = inp["x"].shape[0]
    stages = []
    for i in range(4):
        stages += [("mixer", i), ("ffn", i)]
    stages.append(("final", None))
    if "nc" not in _NC_CACHE:
        _NC_CACHE["nc"] = build(stages)
    nc = _NC_CACHE["nc"]
    shared = host_shared_all(inp)
    maps = []
    for b in range(nb):
        m = dict(shared)
        m.update(host_common(inp, b))
        maps.append(m)
    res = run_bass_kernel_spmd(nc, maps, core_ids=list(range(nb)))
    out = np.stack([np.ascontiguousarray(res.results[b]["out"].T) for b in range(nb)], axis=0)
    return out.astype(np.float32)
```

```python
import contextlib
import numpy as np
import concourse.bass as bass
import concourse.mybir as mybir
from concourse.bass_utils import run_bass_kernel_spmd

F32 = mybir.dt.float32
BF16 = mybir.dt.bfloat16
ALU = mybir.AluOpType
AF = mybir.ActivationFunctionType


class Prog:
    NDS = 8

    def __init__(self):
        self.nc = nc = bass.Bass("TRN2", target_bir_lowering=False)
        self.es = contextlib.ExitStack()
        self.engs = {"pe": nc.tensor, "act": nc.scalar, "dve": nc.vector, "pool": nc.gpsimd, "sp": nc.sync}
        self.sem = {e: self.es.enter_context(nc.semaphore("s_" + e)) for e in self.engs}
        self.cnt = {e: 0 for e in self.engs}
        self.dsem = {q: [self.es.enter_context(nc.semaphore("d_%s%d" % (q, i))) for i in range(self.NDS)]
                     for q in ("sp", "act", "pool")}
        self.dcnt = {q: [0] * self.NDS for q in self.dsem}
        self.dnext = {q: 0 for q in self.dsem}
        self.known = {e: {} for e in self.engs}
        self.track = {}
        self.notrack = set()
        self.ntens = 0
        self.nwaits = 0
        self.ninstr = 0
        self.banks = [self.ps([128, 512], F32, name="bank%d" % i) for i in range(8)]
        self.nbank = 0
        self.rot = list(range(8))

    def sb(self, shape, dt=F32, name=None):
        self.ntens += 1
        return self.es.enter_context(self.nc.sbuf_tensor(name or ("sb%d" % self.ntens), list(shape), dt))

    def ps(self, shape, dt=F32, name=None):
        self.ntens += 1
        return self.es.enter_context(self.nc.psum_tensor(name or ("ps%d" % self.ntens), list(shape), dt))

    def dram(self, name, shape, dt=F32, kind="Internal"):
        t = self.nc.dram_tensor(name, list(shape), dt, kind=kind)
        if kind == "ExternalInput":
            self.notrack.add(name)
        return t

    @staticmethod
    def _region(ap):
        shp = ap.tensor.shape
        rs = 1
        for s in shp[1:]:
            rs *= s
        off = int(ap.offset)
        r0, c0 = off // rs, off % rs
        r1, c1 = r0, c0
        for step, cnt in ap.ap:
            if cnt <= 1 or step == 0:
                continue
            if abs(step) >= rs and step % rs == 0:
                dlt = (cnt - 1) * (step // rs)
                if dlt > 0:
                    r1 += dlt
                else:
                    r0 += dlt
            else:
                dlt = (cnt - 1) * step
                if dlt > 0:
                    c1 += dlt
                else:
                    c0 += dlt
        if ap.name.startswith("bank"):
            return (ap.name, (r0 // 32) * 32, (r1 // 32 + 1) * 32, 0, rs)
        return (ap.name, r0, r1 + 1, c0, c1 + 1)

    @staticmethod
    def _ovl(a, b):
        return a[1] < b[2] and b[1] < a[2] and a[3] < b[4] and b[3] < a[4]

    @staticmethod
    def _contains(a, b):
        return a[1] <= b[1] and b[2] <= a[2] and a[3] <= b[3] and b[4] <= a[4]

    def _deps(self, reads, writes):
        toks = []
        for ap in reads:
            if ap.name in self.notrack:
                continue
            rg = self._region(ap)
            for ent in self.track.get(rg[0], ()):
                if ent["w"] is not None and self._ovl(ent["rg"], rg):
                    toks.append(ent["w"])
        for ap in writes:
            rg = self._region(ap)
            for ent in self.track.get(rg[0], ()):
                if self._ovl(ent["rg"], rg):
                    if ent["w"] is not None:
                        toks.append(ent["w"])
                    toks.extend(ent["r"].items())
        return toks

    def _record(self, reads, writes, tok):
        for ap in writes:
            rg = self._region(ap)
            lst = self.track.setdefault(rg[0], [])
            lst[:] = [e for e in lst if not self._contains(rg, e["rg"])]
            lst.append({"rg": rg, "w": tok, "r": {}})
        for ap in reads:
            if ap.name in self.notrack:
                continue
            rg = self._region(ap)
            lst = self.track.setdefault(rg[0], [])
            for e in lst:
                if e["rg"] == rg:
                    if e["r"].get(tok[0], 0) < tok[1]:
                        e["r"][tok[0]] = tok[1]
                    break
            else:
                lst.append({"rg": rg, "w": None, "r": {tok[0]: tok[1]}})

    def _semh(self, key):
        return self.sem[key[1]] if key[0] == "c" else self.dsem[key[1]][key[2]]

    def _wait(self, eng, toks, skip_pe_self=True):
        best = {}
        for k, v in toks:
            if best.get(k, 0) < v:
                best[k] = v
        kn = self.known[eng]
        for k, v in best.items():
            if eng == "pe" and k == ("c", "pe"):
                continue
            if kn.get(k, 0) >= v:
                continue
            self.engs[eng].wait_ge(self._semh(k), v)
            self.nwaits += 1
            kn[k] = v

    def op(self, eng, fn, reads, writes):
        if eng != "pe":
            pr = [a for a in reads if a.name.startswith("bank")]
            if pr:
                writes = list(writes) + pr
                reads = [a for a in reads if not a.name.startswith("bank")]
        self._wait(eng, self._deps(reads, writes))
        ins = fn(self.engs[eng])
        self.cnt[eng] += 1
        tok = (("c", eng), self.cnt[eng])
        ins.then_inc(self.sem[eng], 1)
        self._record(reads, writes, tok)
        self.ninstr += 1
        return tok

    def dma(self, q, out, in_, **kw):
        i = self.dnext[q]
        self.dnext[q] = (i + 1) % self.NDS
        key = ("d", q, i)
        toks = self._deps([in_], [out])
        if self.dcnt[q][i] > 0:
            toks.append((key, 16 * self.dcnt[q][i]))
        self._wait(q, toks)
        self.engs[q].dma_start(out=out, in_=in_, **kw).then_inc(self.dsem[q][i], 16)
        self.dcnt[q][i] += 1
        tok = (key, 16 * self.dcnt[q][i])
        self._record([in_], [out], tok)
        self.ninstr += 1
        return tok

    def mm(self, out, lhsT, rhs, start=True, stop=True):
        return self.op("pe", lambda e: e.matmul(out, lhsT, rhs, start=start, stop=stop), [lhsT, rhs], [out])

    def actv(self, out, in_, func, bias=None, scale=None, accum_out=None, eng="act"):
        kw = {}
        rd = [in_]
        wr = [out]
        if bias is not None:
            kw["bias"] = bias
            if not isinstance(bias, (int, float)):
                rd.append(bias)
        if scale is not None:
            kw["scale"] = scale
            if not isinstance(scale, (int, float)):
                rd.append(scale)
        if accum_out is not None:
            kw["accum_out"] = accum_out
            wr.append(accum_out)
        return self.op(eng, lambda e: e.activation(out, in_, func, **kw), rd, wr)

    def tt(self, eng, out, in0, in1, op):
        return self.op(eng, lambda e: e.tensor_tensor(out, in0, in1, op), [in0, in1], [out])

    def ts(self, eng, out, in0, s1, s2=None, op0=ALU.mult, op1=None):
        rd = [in0] + [s for s in (s1, s2) if s is not None and not isinstance(s, (int, float))]
        if op1 is None:
            return self.op(eng, lambda e: e.tensor_scalar(out, in0, s1, None, op0), rd, [out])
        return self.op(eng, lambda e: e.tensor_scalar(out, in0, s1, s2, op0, op1), rd, [out])

    def stt(self, eng, out, in0, scalar, in1, op0, op1):
        rd = [in0, in1] + ([] if isinstance(scalar, (int, float)) else [scalar])
        return self.op(eng, lambda e: e.scalar_tensor_tensor(out, in0, scalar, in1, op0, op1), rd, [out])

    def copy(self, eng, out, in_):
        if eng == "act":
            return self.op(eng, lambda e: e.copy(out, in_), [in_], [out])
        return self.op(eng, lambda e: e.tensor_copy(out, in_), [in_], [out])

    def memset(self, eng, ap, val):
        return self.op(eng, lambda e: e.memset(ap, val), [], [ap])

    def scan(self, out, d0, d1, init, op0=ALU.mult, op1=ALU.add):
        rd = [d0, d1] + ([] if isinstance(init, (int, float)) else [init])
        return self.op("dve", lambda e: e.tensor_tensor_scan(out, d0, d1, init, op0, op1), rd, [out])

    def recip(self, out, in_):
        return self.op("dve", lambda e: e.reciprocal(out, in_), [in_], [out])

    def barrier(self):
        toks = [(("c", e), self.cnt[e]) for e in self.engs if self.cnt[e] > 0]
        for q in self.dsem:
            for i in range(self.NDS):
                if self.dcnt[q][i] > 0:
                    toks.append((("d", q, i), 16 * self.dcnt[q][i]))
        for e in self.engs:
            self._wait(e, toks)
        self.track = {}

    @contextlib.contextmanager
    def scope(self):
        outer = self.es
        self.es = contextlib.ExitStack()
        try:
            yield
        finally:
            self.barrier()
            self.es.close()
            self.es = outer

    def bank(self):
        r = self.rot
        b = self.banks[r[self.nbank % len(r)]]
        self.nbank += 1
        return b

    def reserve(self, k):
        self.rot = list(range(k, 8))
        return [self.banks[j] for j in range(k)]

    def unreserve(self):
        self.rot = list(range(8))

    def finish(self, toks):
        self._wait("sp", toks)
        self.es.close()


D = 1024
NCH = 8
TC = 256
TX = 4096
T = TC + TX
TP = T + 3
TILES = [(0, 256)] + [(256 + 512 * j, 512) for j in range(8)]
FFN = 2816
NPAIR = 22
EPS = 1e-6


def pc(t):
    return t + 1 if t < TC else t + 2


def fm(ap2d, lo=None, hi=None):
    v = ap2d.rearrange("(c p) t -> p c t", p=128)
    return v


class Ctx:
    pass


def setup_common(P, G):
    C = Ctx()
    C.ones = P.sb([128, 128], BF16, name="ones_bf")
    P.memset("dve", C.ones[:], 1.0)
    iot = P.sb([128, 128], F32, name="iota_t")
    P.op("pool", lambda e: e.iota(iot[:], [[1, 128]], base=0, channel_multiplier=-1,
                                   allow_small_or_imprecise_dtypes=True), [], [iot[:]])
    C.epsb = P.sb([128, 1], F32, name="epsb")
    P.memset("dve", C.epsb[:], EPS)
    C.iot = iot
    C.ident = P.sb([128, 128], BF16, name="ident_bf")
    P.ts("dve", C.ident[:], iot[:], 0.0, None, ALU.is_equal)
    C.n1g = P.sb([128, 4, 8], F32, name="n1g")
    C.n2g = P.sb([128, 4, 8], F32, name="n2g")
    C.fing = P.sb([128, 8], F32, name="fing")
    C.adab = P.sb([128, 4, 48], F32, name="adab")
    P.dma("sp", C.n1g[:], G["norm1_g"])
    P.dma("sp", C.n2g[:], G["norm2_g"])
    P.dma("sp", C.fing[:], G["final_g"])
    P.dma("sp", C.adab[:], G["ada_b"])
    C.mod = P.sb([128, 4, 48, 2], F32, name="modtab")
    C.gm1 = P.sb([128, 4, 8, 2], F32, name="gm1")
    C.gm2 = P.sb([128, 4, 8, 2], F32, name="gm2")
    with P.scope():
        cond = P.sb([128, 8, 2], F32)
        condb = P.sb([128, 8, 2], BF16)
        P.dma("sp", cond[:], G["cond2"])
        P.actv(condb[:], cond[:], AF.Silu)
        wblk = [P.sb([128, 8, 1024], BF16) for _ in range(2)]
        nb = 0
        for i in G["layers"]:
            ps = P.bank()
            for blk in range(6):
                w = wblk[nb % 2]
                nb += 1
                src = G["ada_w"][i, :, blk * 1024:(blk + 1) * 1024].rearrange("(c p) n -> p c n", p=128)
                P.dma("pool", w[:], src)
                for oc in range(8):
                    col = (blk * 8 + oc) * 2
                    for kc in range(8):
                        P.mm(ps[:, col:col + 2], w[:, kc, oc * 128:(oc + 1) * 128], condb[:, kc, :],
                             start=(kc == 0), stop=(kc == 7))
            P.tt("dve", C.mod[:, i], ps[:, 0:96].rearrange("p (k j) -> p k j", j=2),
                 C.adab[:, i, :].unsqueeze(2).to_broadcast([128, 48, 2]), ALU.add)
            for (gm, ng, base) in ((C.gm1, C.n1g, 8), (C.gm2, C.n2g, 32)):
                P.ts("dve", gm[:, i], C.mod[:, i, base:base + 8, :], 1.0, None, ALU.add)
                P.tt("dve", gm[:, i], gm[:, i], ng[:, i, :].unsqueeze(2).to_broadcast([128, 8, 2]), ALU.mult)
    return C


def norm_mod(P, C, XT, U, gm, sh):
    with P.scope():
        xts = [P.sb([128, 8, 512], F32) for _ in range(2)]
        sqs = [P.sb([128, 8, 512], BF16) for _ in range(2)]
        rs = [P.sb([128, 512], F32) for _ in range(2)]
        tmp = [P.sb([128, 512], F32) for _ in range(4)]
        nt = 0
        for ti, (t0, n) in enumerate(TILES):
            j = 1 if t0 < TC else 0
            xt, sq, r = xts[ti % 2], sqs[ti % 2], rs[ti % 2]
            P.dma("sp", xt[:, :, :n], fm(XT)[:, :, t0:t0 + n])
            P.actv(sq[:, :, :n], xt[:, :, :n], AF.Square)
            ps = P.bank()
            for c in range(8):
                P.mm(ps[:, :n], C.ones[:], sq[:, c, :n], start=(c == 0), stop=(c == 7))
            P.actv(r[:, :n], ps[:, :n], AF.Sqrt, bias=C.epsb[:, 0:1], scale=1.0 / D)
            P.recip(r[:, :n], r[:, :n])
            for c in range(8):
                tm = tmp[nt % 4]
                nt += 1
                P.stt("dve", tm[:, :n], xt[:, c, :n], gm[:, c, j:j + 1], r[:, :n], ALU.mult, ALU.mult)
                P.actv(U[:, c, pc(t0):pc(t0) + n], tm[:, :n], AF.Identity, bias=sh[:, c, j:j + 1], scale=1.0)


def load_w(P, dst, src2d, eng="pool"):
    N = src2d.shape[1]
    v = src2d.rearrange("(c p) n -> p c n", p=128)
    for n0 in range(0, N, 2048):
        n1 = min(N, n0 + 2048)
        P.dma(eng, dst[:, :, n0:n1], v[:, :, n0:n1])


def ffn_block(P, C, XT, G, i):
    MT = G["MT"]
    with P.scope():
        U = P.sb([128, 8, TP], BF16, name="Uf%d" % i)
        P.memset("pool", U[:, :, 0:1], 0.0)
        P.memset("pool", U[:, :, TC + 1:TC + 2], 0.0)
        P.memset("pool", U[:, :, TP - 1:TP], 0.0)
        norm_mod(P, C, XT, U, C.gm2[:, i], C.mod[:, i, 24:32, :])
        cw = P.sb([128, 44, 4], F32)
        P.dma("sp", cw[:], G["ffn_cw"][i])
        wb = [P.sb([128, 8, 256], BF16) for _ in range(2)]
        hb = [P.sb([128, TP], F32) for _ in range(2)]
        acc = [P.sb([128, TP], F32) for _ in range(2)]
        mb = [P.sb([128, TP], BF16) for _ in range(2)]
        for a in acc:
            P.memset("pool", a[:], 0.0)
        wup = G["ffn_w_up"][i]

        def loadw(c):
            w = wb[c % 2]
            for hlf, cc in enumerate((c, c + NPAIR)):
                P.dma("pool", w[:, :, hlf * 128:(hlf + 1) * 128],
                      wup[:, cc * 128:(cc + 1) * 128].rearrange("(k p) n -> p k n", p=128))
        loadw(0)
        for c in range(NPAIR):
            if c + 1 < NPAIR:
                loadw(c + 1)
            w = wb[c % 2]
            for hlf, cc in enumerate((c, c + NPAIR)):
                for c0 in range(0, TP, 512):
                    n = min(512, TP - c0)
                    ps = P.bank()
                    for k in range(8):
                        P.mm(ps[:, :n], w[:, k, hlf * 128:(hlf + 1) * 128], U[:, k, c0:c0 + n],
                             start=(k == 0), stop=(k == 7))
                    P.copy("act", hb[hlf][:, c0:c0 + n], ps[:, :n])
                L = TP - 2
                P.ts("dve", acc[hlf][:, 1:1 + L], hb[hlf][:, 0:L], cw[:, cc, 0:1], cw[:, cc, 3:4], ALU.mult, ALU.add)
                P.stt("dve", acc[hlf][:, 1:1 + L], hb[hlf][:, 1:1 + L], cw[:, cc, 1:2], acc[hlf][:, 1:1 + L], ALU.mult, ALU.add)
                P.stt("dve", acc[hlf][:, 1:1 + L], hb[hlf][:, 2:2 + L], cw[:, cc, 2:3], acc[hlf][:, 1:1 + L], ALU.mult, ALU.add)
            P.actv(acc[1][:], acc[1][:], AF.Silu)
            m = mb[c % 2]
            P.tt("dve", m[:], acc[0][:], acc[1][:], ALU.mult)
            P.dma("sp", MT[c * 128:(c + 1) * 128, :], m[:])
    with P.scope():
        wd = P.sb([128, NPAIR, D], BF16)
        load_w(P, wd, G["ffn_w_down"][i])
        mts = [P.sb([128, NPAIR, 512], BF16) for _ in range(2)]
        xts = [P.sb([128, 8, 512], F32) for _ in range(2)]
        for ti, (t0, n) in enumerate(TILES):
            j = 1 if t0 < TC else 0
            mt, xt = mts[ti % 2], xts[ti % 2]
            P.dma("sp", mt[:, :, :n], MT.rearrange("(k p) t -> p k t", p=128)[:, :, pc(t0):pc(t0) + n])
            P.dma("sp", xt[:, :, :n], fm(XT)[:, :, t0:t0 + n])
            for oc in range(8):
                ps = P.bank()
                for k in range(NPAIR):
                    P.mm(ps[:, :n], wd[:, k, oc * 128:(oc + 1) * 128], mt[:, k, :n], start=(k == 0), stop=(k == NPAIR - 1))
                P.stt("dve", xt[:, oc, :n], ps[:, :n], C.mod[:, i, 40 + oc, j:j + 1], xt[:, oc, :n], ALU.mult, ALU.add)
            P.dma("sp", fm(XT)[:, :, t0:t0 + n], xt[:, :, :n])


def outproj_res(P, C, XT, OT, W2d, i, kch=8):
    with P.scope():
        wo = P.sb([128, kch, D], BF16)
        load_w(P, wo, W2d)
        ots = [P.sb([128, kch, 512], BF16) for _ in range(2)]
        xts = [P.sb([128, 8, 512], F32) for _ in range(2)]
        for ti, (t0, n) in enumerate(TILES):
            j = 1 if t0 < TC else 0
            ot, xt = ots[ti % 2], xts[ti % 2]
            P.dma("sp", ot[:, :, :n], OT.rearrange("(k p) t -> p k t", p=128)[:, :, t0:t0 + n])
            P.dma("sp", xt[:, :, :n], fm(XT)[:, :, t0:t0 + n])
            for oc in range(8):
                ps = P.bank()
                for k in range(kch):
                    P.mm(ps[:, :n], wo[:, k, oc * 128:(oc + 1) * 128], ot[:, k, :n], start=(k == 0), stop=(k == kch - 1))
                P.stt("dve", xt[:, oc, :n], ps[:, :n], C.mod[:, i, 16 + oc, j:j + 1], xt[:, oc, :n], ALU.mult, ALU.add)
            P.dma("sp", fm(XT)[:, :, t0:t0 + n], xt[:, :, :n])


def final_norm(P, C, XT, OUT):
    with P.scope():
        xts = [P.sb([128, 8, 512], F32) for _ in range(2)]
        sqs = [P.sb([128, 8, 512], BF16) for _ in range(2)]
        rs = [P.sb([128, 512], F32) for _ in range(2)]
        toks = []
        for ti, (t0, n) in enumerate(TILES[1:]):
            xt, sq, r = xts[ti % 2], sqs[ti % 2], rs[ti % 2]
            P.dma("sp", xt[:], fm(XT)[:, :, t0:t0 + n])
            P.actv(sq[:], xt[:], AF.Square)
            ps = P.bank()
            for c in range(8):
                P.mm(ps[:], C.ones[:], sq[:, c, :], start=(c == 0), stop=(c == 7))
            P.actv(r[:], ps[:], AF.Sqrt, bias=C.epsb[:, 0:1], scale=1.0 / D)
            P.recip(r[:], r[:])
            for c in range(8):
                P.stt("dve", xt[:, c, :], xt[:, c, :], C.fing[:, c:c + 1], r[:], ALU.mult, ALU.mult)
            toks.append(P.dma("sp", fm(OUT)[:, :, t0 - TC:t0 - TC + n], xt[:]))
        return toks


IN_SPECS = {
    "xT_in": ([D, T], F32),
    "cond2": ([128, 8, 2], F32),
    "ada_w": ([4, D, 6 * D], F32),
    "ada_b": ([128, 4, 48], F32),
    "norm1_g": ([128, 4, 8], F32),
    "norm2_g": ([128, 4, 8], F32),
    "final_g": ([128, 8], F32),
    "ffn_w_up": ([4, D, 2 * FFN], F32),
    "ffn_cw": ([4, 128, 44, 4], F32),
    "ffn_w_down": ([4, FFN, D], F32),
}


def build(stages, out_full=False, extra_inputs=()):
    P = Prog()
    G = {}
    for k, (shp, dt) in IN_SPECS.items():
        G[k] = P.dram(k, shp, dt, kind="ExternalInput").ap()
    for k, (shp, dt) in MIXER_IN_SPECS.items():
        G[k] = P.dram(k, shp, dt, kind="ExternalInput").ap()
    G["layers"] = sorted(set(i for (_, i) in stages if i is not None))
    XT = P.dram("XT", [D, T], F32).ap()
    G["MT"] = P.dram("MT", [FFN, TP], BF16).ap()
    G["OT"] = P.dram("OT", [D, T], BF16).ap()
    if out_full:
        OUT = P.dram("out", [D, T], F32, kind="ExternalOutput").ap()
    else:
        OUT = P.dram("out", [D, TX], F32, kind="ExternalOutput").ap()
    P.dma("sp", XT[:, :], G["xT_in"][:, :])
    C = setup_common(P, G)
    toks = None
    for (kind, i) in stages:
        if kind == "ffn":
            ffn_block(P, C, XT, G, i)
        elif kind == "mixer":
            MIXERS[i % 4](P, C, XT, G, i)
        elif kind == "final":
            toks = final_norm(P, C, XT, OUT)
    if out_full:
        P.barrier()
        toks = [P.dma("sp", OUT[:, :], XT[:, :])]
    P.finish(toks)
    print("program: instr=%d waits=%d" % (P.ninstr, P.nwaits))
    return P.nc


MIXER_IN_SPECS = {}
MIXERS = {}


def host_common(inp, b):
    f = np.float32
    m = {}
    m["xT_in"] = np.ascontiguousarray(np.concatenate([inp["ctx"][b], inp["x"][b]], axis=0).T.astype(f))
    cond2 = np.stack([inp["c"][b], inp["c_ctx"]], axis=-1)
    m["cond2"] = np.ascontiguousarray(cond2.reshape(8, 128, 2).transpose(1, 0, 2))
    return m


def host_shared(inp):
    f = np.float32
    m = {}
    m["ada_w"] = np.ascontiguousarray(inp["ada_w"], dtype=f)
    m["ada_b"] = np.ascontiguousarray(inp["ada_b"].reshape(4, 48, 128).transpose(2, 0, 1), dtype=f)
    m["norm1_g"] = np.ascontiguousarray(inp["norm1_g"].reshape(4, 8, 128).transpose(2, 0, 1), dtype=f)
    m["norm2_g"] = np.ascontiguousarray(inp["norm2_g"].reshape(4, 8, 128).transpose(2, 0, 1), dtype=f)
    m["final_g"] = np.ascontiguousarray(inp["final_g"].reshape(8, 128).T, dtype=f)
    m["ffn_w_up"] = np.ascontiguousarray(inp["ffn_w_up"], dtype=f)
    cw = np.concatenate([inp["ffn_conv_w"], inp["ffn_conv_b"][:, None, :]], axis=1)
    m["ffn_cw"] = np.ascontiguousarray(cw.reshape(4, 4, 44, 128).transpose(0, 3, 2, 1), dtype=f)
    m["ffn_w_down"] = np.ascontiguousarray(inp["ffn_w_down"], dtype=f)
    return m


def host_shared_all(inp):
    m = host_shared(inp)
    for fn in HOST_MIXERS:
        m.update(fn(inp))
    return m


HOST_MIXERS = []


MLA_SCALE = 96 ** -0.5
MIXER_IN_SPECS.update({
    "mla_w_in": ([D, 672], F32),
    "mla_w_in_krp": ([D, 32], F32),
    "mla_qg": ([128, 3], F32),
    "mla_kvg": ([128, 2], F32),
    "mla_w_q_up": ([384, 1536], F32),
    "mla_w_q_rp": ([384, 512], F32),
    "mla_w_kv": ([256, 2048], F32),
    "mla_w_out": ([D, D], F32),
    "rope_tab": ([32, 2, T], F32),
})


def host_mla(inp):
    f = np.float32
    m = {}
    w_in = inp["mla_w_in"][0]
    perm = np.array([j + 8 if (j % 16) < 8 else j - 8 for j in range(32)])
    m["mla_w_in"] = np.ascontiguousarray(w_in, dtype=f)
    m["mla_w_in_krp"] = np.ascontiguousarray(w_in[:, 640 + perm], dtype=f)
    m["mla_qg"] = np.ascontiguousarray(inp["mla_q_norm_g"][0].reshape(3, 128).T, dtype=f)
    m["mla_kvg"] = np.ascontiguousarray(inp["mla_kv_norm_g"][0].reshape(2, 128).T, dtype=f)
    wq = inp["mla_w_q_up"][0]
    m["mla_w_q_up"] = np.ascontiguousarray(wq, dtype=f)
    cols = np.concatenate([h * 96 + 64 + perm for h in range(16)])
    m["mla_w_q_rp"] = np.ascontiguousarray(wq[:, cols], dtype=f)
    wkv = inp["mla_w_kv_up"][0].reshape(256, 16, 128)
    m["mla_w_kv"] = np.ascontiguousarray(np.concatenate([wkv[:, :, :64].reshape(256, 1024), wkv[:, :, 64:].reshape(256, 1024)], axis=1), dtype=f)
    m["mla_w_out"] = np.ascontiguousarray(inp["mla_w_out"][0], dtype=f)
    tab = np.zeros((32, 2, T), f)
    tab[:, 0, :TC] = 1.0
    t = np.arange(TX)
    inv = (1.0 / (10000.0 ** (np.arange(8, dtype=f) / f(8)))).astype(f)
    for j in range(32):
        pos = (t // 64) if j < 16 else (t % 64)
        ang = pos.astype(f) * inv[j % 8]
        tab[j, 0, TC:] = np.cos(ang).astype(f)
        sn = np.sin(ang).astype(f)
        tab[j, 1, TC:] = -sn if (j % 16) < 8 else sn
    m["rope_tab"] = tab
    return m


HOST_MIXERS.append(host_mla)


def run_attn_steps(P, steps, pts, es, state, LOOK=3):
    def issue_qk(st):
        sps = P.bank()
        P.mm(sps[0:st["pr"], :st["n"]], st["K"], st["Q"])
        st["sps"] = sps
    for st in steps[:LOOK]:
        issue_qk(st)
    for i, st in enumerate(steps):
        pr, n, sps = st["pr"], st["n"], st["sps"]
        pt = pts[state["np"] % len(pts)]
        state["np"] += 1
        if st.get("gr") is None:
            P.actv(pt[0:pr, :n], sps[0:pr, :n], AF.Exp)
        else:
            e = es[state["ne"] % len(es)]
            state["ne"] += 1
            P.actv(e[0:pr, :n], sps[0:pr, :n], AF.Exp)
            P.tt("dve", pt[0:pr, :n], e[0:pr, :n], st["gr"], ALU.mult)
        if i + LOOK < len(steps):
            issue_qk(steps[i + LOOK])
        P.mm(st["acc"], st["V"], pt[0:pr, :n], start=st["start"], stop=st["stop"])
        if st.get("fin") is not None:
            st["fin"]()


def make_fin(P, OTd, h, t0, n, ops, ots, rdens, state):
    def fin():
        rd = rdens[state["nf"] % 2]
        ot = ots[state["nf"] % 2]
        state["nf"] += 1
        P.recip(rd[64:128, :n], ops[64:128, :n])
        P.tt("dve", ot[0:64, :n], ops[0:64, :n], rd[64:128, :n], ALU.mult)
        P.dma("sp", OTd[h * 64:(h + 1) * 64, t0:t0 + n], ot[0:64, :n])
    return fin


def dense_steps(P, Kh, Qh, Vh, OTd, h, qtiles, nkt_of, ots, rdens, state):
    dk = state["dk"]
    steps = []
    for (t0, n) in qtiles:
        nkt = nkt_of(t0)
        ops = state["acc"][state["no"] % 2]
        state["no"] += 1
        for kt in range(nkt):
            steps.append(dict(K=Kh[0:dk, kt * 128:(kt + 1) * 128], Q=Qh[0:dk, t0:t0 + n], pr=128, n=n,
                              V=Vh[:, kt, :], acc=ops[:, :n], start=(kt == 0), stop=(kt == nkt - 1),
                              fin=(make_fin(P, OTd, h, t0, n, ops, ots, rdens, state) if kt == nkt - 1 else None)))
    return steps


def mla_mixer(P, C, XT, G, i):
    nc = P.nc
    QT = P.dram("mla_QT", [16, 96, T], BF16).ap()
    KT = P.dram("mla_KT", [16, 64, T], BF16).ap()
    KR = P.dram("mla_KR", [32, T], BF16).ap()
    VT = P.dram("mla_VT", [16, T, 64], BF16).ap()
    OT = G["OT"]
    with P.scope():
        U = P.sb([128, 8, TP], BF16, name="Um%d" % i)
        norm_mod(P, C, XT, U, C.gm1[:, i], C.mod[:, i, 0:8, :])
        w_in = P.sb([128, 8, 672], BF16)
        w_krp = P.sb([128, 8, 32], BF16)
        wq = P.sb([128, 3, 1536], BF16)
        wqp = P.sb([128, 3, 512], BF16)
        wkv = P.sb([128, 2, 2048], BF16)
        load_w(P, w_in, G["mla_w_in"])
        load_w(P, w_krp, G["mla_w_in_krp"])
        load_w(P, wq, G["mla_w_q_up"])
        load_w(P, wqp, G["mla_w_q_rp"])
        load_w(P, wkv, G["mla_w_kv"])
        qg = P.sb([128, 3], F32)
        kvg = P.sb([128, 2], F32)
        P.dma("sp", qg[:], G["mla_qg"])
        P.dma("sp", kvg[:], G["mla_kvg"])
        tabs = [P.sb([96, 2, 512], F32) for _ in range(2)]
        sq = P.sb([128, 3, 512], BF16)
        rstd = P.sb([128, 512], F32)
        qn = P.sb([128, 3, 512], BF16)
        kvn = P.sb([128, 2, 512], BF16)
        t1 = [P.sb([96, 512], F32) for _ in range(2)]
        t2 = [P.sb([96, 512], F32) for _ in range(2)]
        krot = P.sb([32, 512], BF16)
        Qt = [P.sb([96, 16, 512], BF16) for _ in range(2)]
        Kt = [P.sb([128, 8, 512], BF16) for _ in range(2)]
        Vt = [P.sb([128, 1024], BF16) for _ in range(2)]
        nv = 0
        for ti, (t0, n) in enumerate(TILES):
            p0 = pc(t0)
            tab = tabs[ti % 2]
            P.dma("sp", tab[0:32, :, :n], G["rope_tab"][:, :, t0:t0 + n])
            P.dma("sp", tab[64:96, :, :n], G["rope_tab"][:, :, t0:t0 + n])
            for (nch, col0, g, dst, dim) in ((3, 0, qg, qn, 384), (2, 384, kvg, kvn, 256)):
                lat = [P.bank() for _ in range(nch)]
                for c in range(nch):
                    for k in range(8):
                        P.mm(lat[c][:, :n], w_in[:, k, col0 + c * 128:col0 + (c + 1) * 128], U[:, k, p0:p0 + n],
                             start=(k == 0), stop=(k == 7))
                    P.actv(sq[:, c, :n], lat[c][:, :n], AF.Square)
                ss = P.bank()
                for c in range(nch):
                    P.mm(ss[:, :n], C.ones[:], sq[:, c, :n], start=(c == 0), stop=(c == nch - 1))
                P.actv(rstd[:, :n], ss[:, :n], AF.Sqrt, bias=C.epsb[:, 0:1], scale=1.0 / dim)
                P.recip(rstd[:, :n], rstd[:, :n])
                for c in range(nch):
                    P.stt("dve", dst[:, c, :n], lat[c][:, :n], g[:, c:c + 1], rstd[:, :n], ALU.mult, ALU.mult)
            kr = P.bank()
            krp = P.bank()
            for k in range(8):
                P.mm(kr[0:32, :n], w_in[:, k, 640:672], U[:, k, p0:p0 + n], start=(k == 0), stop=(k == 7))
            for k in range(8):
                P.mm(krp[0:32, :n], w_krp[:, k, :], U[:, k, p0:p0 + n], start=(k == 0), stop=(k == 7))
            P.tt("dve", t1[0][0:32, :n], kr[0:32, :n], tab[0:32, 0, :n], ALU.mult)
            P.tt("dve", t2[0][0:32, :n], krp[0:32, :n], tab[0:32, 1, :n], ALU.mult)
            P.tt("dve", krot[:, :n], t1[0][0:32, :n], t2[0][0:32, :n], ALU.add)
            P.dma("sp", KR[:, t0:t0 + n], krot[:, :n])
            qt = Qt[ti % 2]
            for h in range(16):
                qp = P.bank()
                qr = P.bank()
                for k in range(3):
                    P.mm(qp[0:96, :n], wq[:, k, h * 96:(h + 1) * 96], qn[:, k, :n], start=(k == 0), stop=(k == 2))
                for k in range(3):
                    P.mm(qr[0:32, :n], wqp[:, k, h * 32:(h + 1) * 32], qn[:, k, :n], start=(k == 0), stop=(k == 2))
                P.actv(qt[0:64, h, :n], qp[0:64, :n], AF.Copy, scale=MLA_SCALE)
                a, b = t1[h % 2], t2[h % 2]
                P.tt("dve", a[64:96, :n], qp[64:96, :n], tab[64:96, 0, :n], ALU.mult)
                P.stt("dve", b[64:96, :n], qr[0:32, :n], MLA_SCALE, tab[0:32, 1, :n], ALU.mult, ALU.mult)
                P.stt("dve", qt[64:96, h, :n], a[64:96, :n], MLA_SCALE, b[64:96, :n], ALU.mult, ALU.add)
            P.dma("sp", QT[:, :, t0:t0 + n].rearrange("h r t -> r h t"), qt[:, :, :n])
            kt_ = Kt[ti % 2]
            for hp in range(8):
                kp = P.bank()
                for k in range(2):
                    P.mm(kp[:, :n], wkv[:, k, hp * 128:(hp + 1) * 128], kvn[:, k, :n], start=(k == 0), stop=(k == 1))
                P.copy("act", kt_[:, hp, :n], kp[:, :n])
            P.dma("sp", KT.rearrange("(hp two) r t -> (two r) hp t", two=2)[:, :, t0:t0 + n], kt_[:, :, :n])
            for s in range(n // 128):
                vt = Vt[nv % 2]
                nv += 1
                for hf in range(2):
                    vp = P.bank()
                    for k in range(2):
                        P.mm(vp[:, :], kvn[:, k, s * 128:(s + 1) * 128], wkv[:, k, 1024 + hf * 512:1024 + (hf + 1) * 512],
                             start=(k == 0), stop=(k == 1))
                    P.copy("act", vt[:, hf * 512:(hf + 1) * 512], vp[:, :])
                tt0 = t0 + s * 128
                P.dma("sp", VT[:, tt0:tt0 + 128, :].rearrange("h t d -> t h d"), vt[:].rearrange("p (h d) -> p h d", d=64))
    with P.scope():
        Khs = [P.sb([96, T], BF16) for _ in range(2)]
        Qhs = [P.sb([96, T], BF16) for _ in range(2)]
        Vhs = [P.sb([128, 34, 128], BF16) for _ in range(2)]
        for v in Vhs:
            P.memset("pool", v[:, :, 64:128], 1.0)
        pts = [P.sb([128, 512], BF16) for _ in range(6)]
        ots = [P.sb([64, 512], BF16) for _ in range(2)]
        rdens = [P.sb([128, 512], F32) for _ in range(2)]
        state = {"np": 0, "no": 0, "nf": 0, "ne": 0, "dk": 96, "acc": P.reserve(2)}

        def loadh(h):
            Kh, Qh, Vh = Khs[h % 2], Qhs[h % 2], Vhs[h % 2]
            P.dma("sp", Kh[0:64, :], KT[h])
            P.dma("sp", Kh[64:96, :], KR[:, :])
            P.dma("sp", Qh[:, :], QT[h])
            P.dma("sp", Vh[:, :, 0:64], VT[h].rearrange("(kt p) d -> p kt d", p=128))
        loadh(0)
        for h in range(16):
            if h + 1 < 16:
                loadh(h + 1)
            steps = dense_steps(P, Khs[h % 2], Qhs[h % 2], Vhs[h % 2], OT, h, TILES, lambda t0: 2 if t0 < TC else 34,
                                ots, rdens, state)
            run_attn_steps(P, steps, pts, None, state)
        P.unreserve()
    outproj_res(P, C, XT, OT, G["mla_w_out"], i)


MIXERS[0] = mla_mixer


NA_SCALE = 64 ** -0.5
MIXER_IN_SPECS.update({
    "na_w_qkv": ([D, 3 * D], F32),
    "na_w_out": ([D, D], F32),
    "na_rpb_g": ([64, 16, 15, 64], F32),
    "na_colmask": ([64, 64], F32),
})


def host_natten(inp):
    f = np.float32
    m = {}
    m["na_w_qkv"] = np.ascontiguousarray(inp["na_w_qkv"][0], dtype=f)
    m["na_w_out"] = np.ascontiguousarray(inp["na_w_out"][0], dtype=f)
    rpb = inp["na_rpb"][0]
    wk = np.arange(64)[:, None]
    wq = np.arange(64)[None, :]
    dc = wk - wq + 15
    ok = (dc >= 0) & (dc <= 30)
    dcc = np.clip(dc, 0, 30)
    g = rpb[:, ::-1, :][:, :, dcc]
    g = np.where(ok[None, None], g, f(0))
    m["na_rpb_g"] = np.ascontiguousarray(g.transpose(2, 0, 1, 3), dtype=f)
    c0 = np.clip(np.arange(64) - 8, 0, 48)[None, :]
    m["na_colmask"] = ((wk >= c0) & (wk <= c0 + 15)).astype(f)
    return m


HOST_MIXERS.append(host_natten)


def natten_mixer(P, C, XT, G, i):
    QT = P.dram("na_QT", [16, 64, T], BF16).ap()
    KT = P.dram("na_KT", [16, 64, T], BF16).ap()
    VT = P.dram("na_VT", [16, T, 64], BF16).ap()
    OT = G["OT"]
    with P.scope():
        U = P.sb([128, 8, TP], BF16, name="Un%d" % i)
        norm_mod(P, C, XT, U, C.gm1[:, i], C.mod[:, i, 0:8, :])
        w = P.sb([128, 8, 3 * D], BF16)
        load_w(P, w, G["na_w_qkv"])
        Qt = [P.sb([128, 8, 512], BF16) for _ in range(2)]
        Kt = [P.sb([128, 8, 512], BF16) for _ in range(2)]
        Vt = [P.sb([128, 1024], BF16) for _ in range(2)]
        nv = 0
        for ti, (t0, n) in enumerate(TILES):
            p0 = pc(t0)
            for (dst, DT, cbase, scl) in ((Qt[ti % 2], QT, 0, NA_SCALE), (Kt[ti % 2], KT, D, 1.0)):
                for hp in range(8):
                    ps = P.bank()
                    for k in range(8):
                        P.mm(ps[:, :n], w[:, k, cbase + hp * 128:cbase + (hp + 1) * 128], U[:, k, p0:p0 + n],
                             start=(k == 0), stop=(k == 7))
                    P.actv(dst[:, hp, :n], ps[:, :n], AF.Copy, scale=scl)
                P.dma("sp", DT.rearrange("(hp two) r t -> (two r) hp t", two=2)[:, :, t0:t0 + n], dst[:, :, :n])
            for s in range(n // 128):
                vt = Vt[nv % 2]
                nv += 1
                for hf in range(2):
                    vp = P.bank()
                    for k in range(8):
                        P.mm(vp[:, :], U[:, k, p0 + s * 128:p0 + (s + 1) * 128],
                             w[:, k, 2 * D + hf * 512:2 * D + (hf + 1) * 512], start=(k == 0), stop=(k == 7))
                    P.copy("act", vt[:, hf * 512:(hf + 1) * 512], vp[:, :])
                tt0 = t0 + s * 128
                P.dma("sp", VT[:, tt0:tt0 + 128, :].rearrange("h t d -> t h d"), vt[:].rearrange("p (h d) -> p h d", d=64))
    with P.scope():
        GR = P.sb([64, 16, 15 * 64], BF16)
        cm = P.sb([64, 64], F32)
        P.dma("sp", cm[:], G["na_colmask"])
        rp = [P.sb([64, 15, 64], F32) for _ in range(2)]
        for h in range(16):
            r = rp[h % 2]
            P.dma("sp", r[:], G["na_rpb_g"][:, h])
            P.actv(r[:], r[:], AF.Exp)
            P.tt("dve", GR[:, h, :].rearrange("p (k w) -> p k w", w=64), r[:],
                 cm[:].unsqueeze(1).to_broadcast([64, 15, 64]), ALU.mult)
        Khs = [P.sb([64, T], BF16) for _ in range(2)]
        Qhs = [P.sb([64, T], BF16) for _ in range(2)]
        Vcs = [P.sb([128, 2, 128], BF16) for _ in range(2)]
        Vgs = [P.sb([64, 64, 128], BF16) for _ in range(2)]
        for v in Vcs + Vgs:
            P.memset("pool", v[:, :, 64:128], 1.0)
        pts = [P.sb([128, 512], BF16) for _ in range(6)]
        es = [P.sb([64, 512], F32) for _ in range(4)]
        ots = [P.sb([64, 512], BF16) for _ in range(2)]
        rdens = [P.sb([128, 512], F32) for _ in range(2)]
        state = {"np": 0, "no": 0, "nf": 0, "ne": 0, "dk": 64, "acc": P.reserve(2)}

        def loadh(h):
            P.dma("sp", Khs[h % 2][:, :], KT[h])
            P.dma("sp", Qhs[h % 2][:, :], QT[h])
            P.dma("sp", Vcs[h % 2][:, :, 0:64], VT[h, 0:TC, :].rearrange("(kt p) d -> p kt d", p=128))
            P.dma("sp", Vgs[h % 2][:, :, 0:64], VT[h, TC:T, :].rearrange("(r w) d -> w r d", w=64))
        loadh(0)
        for h in range(16):
            if h + 1 < 16:
                loadh(h + 1)
            Kh, Qh, Vc, Vg = Khs[h % 2], Qhs[h % 2], Vcs[h % 2], Vgs[h % 2]
            steps = dense_steps(P, Kh, Qh, Vc, OT, h, [(0, TC)], lambda t0: 2, ots, rdens, state)
            for qb in range(8):
                t0 = TC + 512 * qb
                ops = state["acc"][state["no"] % 2]
                state["no"] += 1
                for kt in range(2):
                    steps.append(dict(K=Kh[:, kt * 128:(kt + 1) * 128], Q=Qh[:, t0:t0 + 512], pr=128, n=512,
                                      V=Vc[:, kt, :], acc=ops[:, :], start=(kt == 0), stop=False))
                rows = []
                for rk in range(64):
                    val = [r for r in range(8 * qb, 8 * qb + 8) if min(max(r - 4, 0), 56) <= rk <= min(max(r - 4, 0), 56) + 7]
                    if val:
                        rows.append((rk, val[0], len(val)))
                for idx, (rk, ra, nj) in enumerate(rows):
                    q0 = (ra - 8 * qb) * 64
                    nq = nj * 64
                    ka = 7 - rk + ra
                    last = (idx == len(rows) - 1)
                    steps.append(dict(K=Kh[:, TC + rk * 64:TC + (rk + 1) * 64], Q=Qh[:, t0 + q0:t0 + q0 + nq], pr=64, n=nq,
                                      V=Vg[:, rk, :], acc=ops[:, q0:q0 + nq], start=False, stop=last,
                                      gr=GR[:, h, ka * 64:ka * 64 + nq],
                                      fin=(make_fin(P, OT, h, t0, 512, ops, ots, rdens, state) if last else None)))
            run_attn_steps(P, steps, pts, es, state, LOOK=4)
        P.unreserve()
    outproj_res(P, C, XT, OT, G["na_w_out"], i)


MIXERS[3] = natten_mixer


MIXER_IN_SPECS.update({
    "s5_w_in": ([D, 512], F32),
    "s5_par": ([128, 3, 32], F32),
    "s5_B": ([128, 2, 2, 16, 128], F32),
    "s5_C": ([128, 2, 2, 16, 32], F32),
    "s5_dsk": ([128, 4], F32),
    "s5_w_glu": ([512, 2 * D], F32),
})


def host_s5(inp):
    f = np.float32
    m = {}
    m["s5_w_in"] = np.ascontiguousarray(inp["s5_w_in"][0], dtype=f)
    lre, lim, ldt = inp["s5_lambda_re"][0], inp["s5_lambda_im"][0], inp["s5_log_dt"][0]
    par = np.zeros((128, 3, 2, 16), f)
    B = np.zeros((128, 2, 2, 16, 128), f)
    Cm = np.zeros((128, 2, 2, 16, 32), f)
    bre, bim = inp["s5_b_re"][0], inp["s5_b_im"][0]
    cre, cim = inp["s5_c_re"][0], inp["s5_c_im"][0]
    for g in range(32):
        s, gh = g // 2, g % 2
        for d in range(2):
            par[gh * 64:(gh + 1) * 64, 0, d, s] = lre[d, g]
            par[gh * 64:(gh + 1) * 64, 1, d, s] = lim[d, g]
            par[gh * 64:(gh + 1) * 64, 2, d, s] = ldt[d, g]
            pp = (s % 4) * 32 + gh * 16
            B[pp:pp + 16, 0, d, s, gh * 64:(gh + 1) * 64] = bre[d, g].T
            B[pp:pp + 16, 1, d, s, gh * 64:(gh + 1) * 64] = bim[d, g].T
            Cm[gh * 64:(gh + 1) * 64, 0, d, s, gh * 16:(gh + 1) * 16] = cre[d, g].T
            Cm[gh * 64:(gh + 1) * 64, 1, d, s, gh * 16:(gh + 1) * 16] = cim[d, g].T
    m["s5_par"] = par.reshape(128, 3, 32)
    m["s5_B"] = B
    m["s5_C"] = Cm
    m["s5_dsk"] = np.ascontiguousarray(inp["s5_d"][0].reshape(4, 128).T, dtype=f)
    m["s5_w_glu"] = np.ascontiguousarray(inp["s5_w_glu"][0], dtype=f)
    return m


HOST_MIXERS.append(host_s5)


def _horner(P, out, x2, coefs, tmpa):
    cs = list(coefs)
    P.ts("dve", out, x2, cs[-1], cs[-2], ALU.mult, ALU.add)
    for c in reversed(cs[:-2]):
        P.tt("dve", tmpa, out, x2, ALU.mult)
        P.ts("dve", out, tmpa, c, None, ALU.add)


def s5_mixer(P, C, XT, G, i):
    import math
    DBG = 0
    ZT = P.dram("s5_ZT", [512, T], F32).ap()
    YT = P.dram("s5_YT", [512, T], F32).ap()
    I32 = mybir.dt.int32
    with P.scope():
        zb = P.sb([128, 4, T], BF16, name="s5_zb")
        with P.scope():
            U = P.sb([128, 8, TP], BF16, name="Us%d" % i)
            norm_mod(P, C, XT, U, C.gm1[:, i], C.mod[:, i, 0:8, :])
            w = P.sb([128, 8, 512], BF16)
            load_w(P, w, G["s5_w_in"])
            zf = [P.sb([128, 4, 512], F32) for _ in range(2)]
            for ti, (t0, n) in enumerate(TILES):
                p0 = pc(t0)
                z = zf[ti % 2]
                for c in range(4):
                    ps = P.bank()
                    for k in range(8):
                        P.mm(ps[:, :n], w[:, k, c * 128:(c + 1) * 128], U[:, k, p0:p0 + n], start=(k == 0), stop=(k == 7))
                    P.copy("act", z[:, c, :n], ps[:, :n])
                    P.copy("dve", zb[:, c, t0:t0 + n], ps[:, :n])
                P.dma("sp", ZT.rearrange("(c p) t -> p c t", p=128)[:, :, t0:t0 + n], z[:, :, :n])
        par = P.sb([128, 3, 32], F32)
        P.dma("sp", par[:], G["s5_par"])
        Bb = P.sb([128, 2, 2, 16, 128], BF16)
        with P.scope():
            Bf = P.sb([128, 2, 2, 16, 128], F32)
            P.dma("sp", Bf[:], G["s5_B"])
            P.copy("dve", Bb[:], Bf[:])
        Cf = P.sb([128, 2, 2, 16, 32], F32)
        P.dma("sp", Cf[:], G["s5_C"])
        Cc = P.sb([128, 2, 32, 32], BF16)
        sm = [P.sb([128, 32], F32, name="s5sm%d" % j) for j in range(24)]
        ki = P.sb([128, 32], I32)
        lre, lim, ldt = par[:, 0, :], par[:, 1, :], par[:, 2, :]
        dt, xx, mag, ang, kf, r, x, x2, ps_, pc_, ta, s2, sn, cs, are, aim, num_re, num_im, den, cfr, cfi, ncfr, ncfi, tb = [t[:] for t in sm]
        P.actv(dt, ldt, AF.Exp)
        P.tt("dve", xx, lre, dt, ALU.mult)
        _horner(P, mag, xx, [1.0, 1.0, 1 / 2., 1 / 6., 1 / 24., 1 / 120., 1 / 720., 1 / 5040.], ta)
        P.tt("dve", ang, lim, dt, ALU.mult)
        P.ts("dve", kf, ang, 1.0 / (2 * math.pi), None, ALU.mult)
        P.copy("dve", ki[:], kf)
        P.copy("dve", kf, ki[:])
        P.stt("dve", r, kf, -6.28125, ang, ALU.mult, ALU.add)
        P.stt("dve", r, kf, -(2 * math.pi - 6.28125), r, ALU.mult, ALU.add)
        P.ts("dve", x, r, 0.5, None, ALU.mult)
        P.tt("dve", x2, x, x, ALU.mult)
        f_ = math.factorial
        _horner(P, ps_, x2, [1.0, -1. / f_(3), 1. / f_(5), -1. / f_(7), 1. / f_(9), -1. / f_(11), 1. / f_(13)], ta)
        _horner(P, pc_, x2, [1.0, -1. / f_(2), 1. / f_(4), -1. / f_(6), 1. / f_(8), -1. / f_(10), 1. / f_(12), -1. / f_(14)], ta)
        P.tt("dve", s2, ps_, x, ALU.mult)
        P.tt("dve", sn, s2, pc_, ALU.mult)
        P.ts("dve", sn, sn, 2.0, None, ALU.mult)
        P.tt("dve", cs, s2, s2, ALU.mult)
        P.ts("dve", cs, cs, -2.0, 1.0, ALU.mult, ALU.add)
        P.tt("dve", are, mag, cs, ALU.mult)
        P.tt("dve", aim, mag, sn, ALU.mult)
        P.ts("dve", ta, are, -1.0, None, ALU.add)
        P.tt("dve", num_re, ta, lre, ALU.mult)
        P.tt("dve", tb, aim, lim, ALU.mult)
        P.tt("dve", num_re, num_re, tb, ALU.add)
        P.tt("dve", num_im, aim, lre, ALU.mult)
        P.tt("dve", tb, ta, lim, ALU.mult)
        P.tt("dve", num_im, num_im, tb, ALU.subtract)
        P.tt("dve", den, lre, lre, ALU.mult)
        P.tt("dve", tb, lim, lim, ALU.mult)
        P.tt("dve", den, den, tb, ALU.add)
        P.recip(den, den)
        P.tt("dve", cfr, num_re, den, ALU.mult)
        P.tt("dve", cfi, num_im, den, ALU.mult)
        P.ts("dve", ncfr, cfr, -1.0, None, ALU.mult)
        P.ts("dve", ncfi, cfi, -1.0, None, ALU.mult)
        ct = P.sb([128, 32], F32)
        for d in range(2):
            for s in range(16):
                j = d * 16 + s
                P.ts("dve", ct[:], Cf[:, 0, d, s, :], cfr[:, j:j + 1], None, ALU.mult)
                P.stt("dve", Cc[:, 0, j, :], Cf[:, 1, d, s, :], ncfi[:, j:j + 1], ct[:], ALU.mult, ALU.add)
                P.ts("dve", ct[:], Cf[:, 0, d, s, :], ncfi[:, j:j + 1], None, ALU.mult)
                P.stt("dve", Cc[:, 1, j, :], Cf[:, 1, d, s, :], ncfr[:, j:j + 1], ct[:], ALU.mult, ALU.add)
        E = P.sb([128, 2, T], F32, name="s5_E")
        Gb = P.sb([128, 2, T], F32, name="s5_G")
        H = [[P.sb([128, T], BF16) for _ in range(2)] for _ in range(2)]
        pw = P.sb([128, 4], F32)
        tm = [P.sb([128, 512], F32) for _ in range(4)]
        ysb = [P.sb([32, 512], F32) for _ in range(2)]
        ny = 0

        def kidx(t):
            return (TC - 1 - t) if t < TC else (TC + T - 1 - t)

        def ev(c, t0, n, d):
            if d == 0:
                return E[:, c, t0:t0 + n]
            lo, hi = kidx(t0 + n - 1), kidx(t0)
            return E[:, c, lo:hi + 1][:, ::-1]

        for s in range(16 if DBG == 0 else 1):
            for d in range(2):
                j = d * 16 + s
                if DBG == 1:
                    continue
                P.memset("dve", E[:, 0, 0:1], 1.0)
                P.memset("dve", E[:, 1, 0:1], 0.0)
                P.copy("dve", pw[:, 0:1], cs[:, j:j + 1])
                P.copy("dve", pw[:, 1:2], sn[:, j:j + 1])
                mlen = 1
                while mlen < T:
                    L = min(mlen, T - mlen)
                    P.ts("dve", pw[:, 2:3], pw[:, 1:2], -1.0, None, ALU.mult)
                    P.ts("dve", E[:, 0, mlen:mlen + L], E[:, 0, 0:L], pw[:, 0:1], None, ALU.mult)
                    P.stt("dve", E[:, 0, mlen:mlen + L], E[:, 1, 0:L], pw[:, 2:3], E[:, 0, mlen:mlen + L], ALU.mult, ALU.add)
                    P.ts("dve", E[:, 1, mlen:mlen + L], E[:, 0, 0:L], pw[:, 1:2], None, ALU.mult)
                    P.stt("dve", E[:, 1, mlen:mlen + L], E[:, 1, 0:L], pw[:, 0:1], E[:, 1, mlen:mlen + L], ALU.mult, ALU.add)
                    mlen *= 2
                    if mlen < T:
                        P.tt("dve", pw[:, 3:4], pw[:, 1:2], pw[:, 1:2], ALU.mult)
                        P.stt("dve", pw[:, 1:2], pw[:, 1:2], 2.0, pw[:, 0:1], ALU.mult, ALU.mult)
                        P.tt("dve", pw[:, 0:1], pw[:, 0:1], pw[:, 0:1], ALU.mult)
                        P.tt("dve", pw[:, 0:1], pw[:, 0:1], pw[:, 3:4], ALU.subtract)
                if DBG == 2:
                    continue
                pb = (s % 4) * 32
                for (t0, n) in TILES:
                    pre, pim = P.bank(), P.bank()
                    P.mm(pre[:, :n], Bb[:, 0, d, s, :], zb[:, s // 4, t0:t0 + n])
                    P.mm(pim[:, :n], Bb[:, 1, d, s, :], zb[:, s // 4, t0:t0 + n])
                    er, ei = ev(0, t0, n, d), ev(1, t0, n, d)
                    P.tt("dve", tm[0][:, :n], pre[:, :n], er, ALU.mult)
                    P.tt("dve", tm[1][:, :n], pim[:, :n], ei, ALU.mult)
                    P.tt("pool", Gb[:, 0, t0:t0 + n], tm[0][:, :n], tm[1][:, :n], ALU.add)
                    P.tt("dve", tm[2][:, :n], pim[:, :n], er, ALU.mult)
                    P.tt("dve", tm[3][:, :n], pre[:, :n], ei, ALU.mult)
                    P.tt("pool", Gb[:, 1, t0:t0 + n], tm[2][:, :n], tm[3][:, :n], ALU.subtract)
                if DBG == 3:
                    continue
                for c in range(2):
                    if d == 0:
                        P.scan(Gb[:, c, :], mag[:, j:j + 1].to_broadcast([128, T]), Gb[:, c, :], 0.0)
                    else:
                        P.scan(Gb[:, c, 0:TC][:, ::-1], mag[:, j:j + 1].to_broadcast([128, TC]), Gb[:, c, 0:TC][:, ::-1], 0.0)
                        P.scan(Gb[:, c, TC:T][:, ::-1], mag[:, j:j + 1].to_broadcast([128, TX]), Gb[:, c, TC:T][:, ::-1],
                               Gb[:, c, 0:1])
                if DBG == 4:
                    continue
                for (t0, n) in TILES:
                    er, ei = ev(0, t0, n, d), ev(1, t0, n, d)
                    gr, gi = Gb[:, 0, t0:t0 + n], Gb[:, 1, t0:t0 + n]
                    P.tt("dve", tm[0][:, :n], er, gr, ALU.mult)
                    P.tt("dve", tm[1][:, :n], ei, gi, ALU.mult)
                    P.tt("pool", H[d][0][:, t0:t0 + n], tm[0][:, :n], tm[1][:, :n], ALU.subtract)
                    P.tt("dve", tm[2][:, :n], er, gi, ALU.mult)
                    P.tt("dve", tm[3][:, :n], ei, gr, ALU.mult)
                    P.tt("pool", H[d][1][:, t0:t0 + n], tm[2][:, :n], tm[3][:, :n], ALU.add)
            for (t0, n) in (TILES if DBG in (0, 6) else []):
                yp = P.bank()
                P.mm(yp[0:32, :n], Cc[:, 0, s, :], H[0][0][:, t0:t0 + n], start=True, stop=False)
                P.mm(yp[0:32, :n], Cc[:, 1, s, :], H[0][1][:, t0:t0 + n], start=False, stop=False)
                P.mm(yp[0:32, :n], Cc[:, 0, 16 + s, :], H[1][0][:, t0:t0 + n], start=False, stop=False)
                P.mm(yp[0:32, :n], Cc[:, 1, 16 + s, :], H[1][1][:, t0:t0 + n], start=False, stop=True)
                y = ysb[ny % 2]
                ny += 1
                P.copy("act", y[:, :n], yp[0:32, :n])
                P.dma("sp", YT[s * 32:(s + 1) * 32, t0:t0 + n], y[:, :n])
    with P.scope():
        wg = P.sb([128, 4, 2 * D], BF16)
        load_w(P, wg, G["s5_w_glu"])
        dsk = P.sb([128, 4], F32)
        P.dma("sp", dsk[:], G["s5_dsk"])
        ys = [P.sb([128, 4, 512], F32) for _ in range(2)]
        zs = [P.sb([128, 4, 512], F32) for _ in range(2)]
        t1 = P.sb([128, 4, 512], F32)
        yb = [P.sb([128, 4, 512], BF16) for _ in range(2)]
        xts = [P.sb([128, 8, 512], F32) for _ in range(2)]
        sg = [P.sb([128, 512], F32) for _ in range(2)]
        for ti, (t0, n) in enumerate(TILES):
            jx = 1 if t0 < TC else 0
            y, z, xt, ybb = ys[ti % 2], zs[ti % 2], xts[ti % 2], yb[ti % 2]
            P.dma("sp", y[:, :, :n], YT.rearrange("(c p) t -> p c t", p=128)[:, :, t0:t0 + n])
            P.dma("sp", z[:, :, :n], ZT.rearrange("(c p) t -> p c t", p=128)[:, :, t0:t0 + n])
            P.dma("sp", xt[:, :, :n], fm(XT)[:, :, t0:t0 + n])
            for c in range(4):
                P.stt("dve", y[:, c, :n], z[:, c, :n], dsk[:, c:c + 1], y[:, c, :n], ALU.mult, ALU.add)
            P.actv(t1[:, :, :n], y[:, :, :n], AF.Square)
            P.ts("dve", t1[:, :, :n], t1[:, :, :n], 0.044715, 1.0, ALU.mult, ALU.add)
            P.tt("dve", t1[:, :, :n], t1[:, :, :n], y[:, :, :n], ALU.mult)
            P.actv(t1[:, :, :n], t1[:, :, :n], AF.Sigmoid, scale=1.5957691216057308)
            P.tt("dve", ybb[:, :, :n], t1[:, :, :n], y[:, :, :n], ALU.mult)
            for oc in range(8):
                pa, pg = P.bank(), P.bank()
                for k in range(4):
                    P.mm(pa[:, :n], wg[:, k, oc * 128:(oc + 1) * 128], ybb[:, k, :n], start=(k == 0), stop=(k == 3))
                for k in range(4):
                    P.mm(pg[:, :n], wg[:, k, D + oc * 128:D + (oc + 1) * 128], ybb[:, k, :n], start=(k == 0), stop=(k == 3))
                s_ = sg[oc % 2]
                P.actv(s_[:, :n], pg[:, :n], AF.Sigmoid)
                P.tt("dve", s_[:, :n], pa[:, :n], s_[:, :n], ALU.mult)
                P.stt("dve", xt[:, oc, :n], s_[:, :n], C.mod[:, i, 16 + oc, jx:jx + 1], xt[:, oc, :n], ALU.mult, ALU.add)
            P.dma("sp", fm(XT)[:, :, t0:t0 + n], xt[:, :, :n])


MIXERS[1] = s5_mixer


HG_SCALE = 128 ** -0.5
MIXER_IN_SPECS.update({
    "hg_w_in": ([D, 5 * D], F32),
    "hg_w_out": ([D, D], F32),
    "hg_ng": ([128, 8], F32),
    "hg_lbraw": ([128, 8, 4], F32),
})


def host_hg(inp):
    f = np.float32
    m = {}
    m["hg_w_in"] = np.ascontiguousarray(inp["hg_w_in"][0], dtype=f)
    m["hg_w_out"] = np.ascontiguousarray(inp["hg_w_out"][0], dtype=f)
    m["hg_ng"] = np.ascontiguousarray(inp["hg_norm_g"][0].reshape(8, 128).T, dtype=f)
    m["hg_lbraw"] = np.ascontiguousarray(inp["hg_lower_bound"].reshape(4, 8, 128).transpose(2, 1, 0), dtype=f)
    return m


HOST_MIXERS.append(host_hg)


def hgrn2_mixer(P, C, XT, G, i):
    QF = P.dram("hg_QF", [D, T], F32).ap()
    FF = [P.dram("hg_F%d" % d, [D, T], F32).ap() for d in range(2)]
    GS = P.dram("hg_GS", [D, T], F32).ap()
    VH = P.dram("hg_VH", [T, D], BF16).ap()
    OT = G["OT"]
    NCK = T // 64
    HDBG = 0
    with P.scope():
        lbr = P.sb([128, 8, 4], F32)
        P.dma("sp", lbr[:], G["hg_lbraw"])
        P.actv(lbr[:], lbr[:], AF.Exp)
        ssum = P.sb([128, 8], F32)
        lb = P.sb([128, 8], F32)
        oml = P.sb([128, 8], F32)
        ng = P.sb([128, 8], F32)
        P.dma("sp", ng[:], G["hg_ng"])
        P.tt("dve", ssum[:], lbr[:, :, 0], lbr[:, :, 1], ALU.add)
        P.tt("dve", ssum[:], ssum[:], lbr[:, :, 2], ALU.add)
        P.tt("dve", ssum[:], ssum[:], lbr[:, :, 3], ALU.add)
        P.recip(ssum[:], ssum[:])
        P.memset("dve", lb[:], 0.0)
        for l in range(1, i + 1):
            P.tt("dve", lb[:], lb[:], lbr[:, :, l], ALU.add)
        P.tt("dve", lb[:], lb[:], ssum[:], ALU.mult)
        P.ts("dve", oml[:], lb[:], -1.0, 1.0, ALU.mult, ALU.add)
        with P.scope():
            U = P.sb([128, 8, TP], BF16, name="Uh%d" % i)
            norm_mod(P, C, XT, U, C.gm1[:, i], C.mod[:, i, 0:8, :])
            wbs = [P.sb([128, 8, D], BF16) for _ in range(2)]
            st = [P.sb([128, 8, 512], F32) for _ in range(2)]
            sgm = [P.sb([128, 512], F32) for _ in range(2)]
            Vt = [P.sb([128, D], BF16) for _ in range(2)]
            nst = 0
            nv = 0
            for bi, blk in enumerate((0, 1, 2, 4, 3)):
                wb = wbs[bi % 2]
                load_w(P, wb, G["hg_w_in"][:, blk * D:(blk + 1) * D])
                for (t0, n) in TILES:
                    p0 = pc(t0)
                    if blk == 3:
                        for s in range(n // 128):
                            vt = Vt[nv % 2]
                            nv += 1
                            for hf in range(2):
                                vp = P.bank()
                                for k in range(8):
                                    P.mm(vp[:, :], U[:, k, p0 + s * 128:p0 + (s + 1) * 128], wb[:, k, hf * 512:(hf + 1) * 512],
                                         start=(k == 0), stop=(k == 7))
                                P.copy("act", vt[:, hf * 512:(hf + 1) * 512], vp[:, :])
                            tt0 = t0 + s * 128
                            P.dma("sp", VH[tt0:tt0 + 128, :], vt[:])
                        continue
                    stg = st[nst % 2]
                    nst += 1
                    for c in range(8):
                        ps = P.bank()
                        for k in range(8):
                            P.mm(ps[:, :n], wb[:, k, c * 128:(c + 1) * 128], U[:, k, p0:p0 + n], start=(k == 0), stop=(k == 7))
                        if blk == 0:
                            P.actv(stg[:, c, :n], ps[:, :n], AF.Silu)
                        elif blk == 4:
                            sg = sgm[c % 2]
                            P.actv(sg[:, :n], ps[:, :n], AF.Silu)
                            P.ts("dve", stg[:, c, :n], sg[:, :n], ng[:, c:c + 1], None, ALU.mult)
                        else:
                            sg = sgm[c % 2]
                            P.actv(sg[:, :n], ps[:, :n], AF.Sigmoid)
                            P.ts("dve", stg[:, c, :n], sg[:, :n], oml[:, c:c + 1], lb[:, c:c + 1], ALU.mult, ALU.add)
                    dstT = {0: QF, 1: FF[0], 2: FF[1], 4: GS}[blk]
                    P.dma("sp", fm(dstT)[:, :, t0:t0 + n], stg[:, :, :n])
        with P.scope():
            rm = [P.sb([128, T], F32) for _ in range(2)]
            P.memset("pool", rm[0][:], 1.0)
            P.memset("pool", rm[1][:], 1.0)
            P.memset("pool", rm[0][:, 0:T:64], 0.0)
            P.memset("pool", rm[1][:, 63:T:64], 0.0)
            msk = [P.sb([64, 64], F32) for _ in range(2)]
            P.ts("dve", msk[0][:], C.iot[0:64, 0:64], 0.0, None, ALU.is_ge)
            P.ts("dve", msk[1][:], C.iot[0:64, 0:64], 0.0, None, ALU.is_le)
            q = P.sb([128, T], F32)
            A = P.sb([128, T], F32)
            Bq = P.sb([128, T], F32)
            Cb = P.sb([128, T], F32)
            qin = [P.sb([128, T], BF16) for _ in range(2)]
            kin = [P.sb([128, T], BF16) for _ in range(2)]
            kout = [P.sb([128, T], BF16) for _ in range(2)]
            ebl = [P.sb([128, NCK], F32) for _ in range(2)]
            v = P.sb([64, NCK, 128], BF16)
            oacc = P.sb([128, T], F32)
            S = [P.sb([128, 128], F32) for _ in range(2)]
            Sb = [P.sb([128, 128], BF16) for _ in range(2)]
            attm = [P.sb([64, 64], BF16) for _ in range(4)]
            ko = [P.sb([64, 128], BF16) for _ in range(4)]
            sq = [P.sb([128, 512], BF16) for _ in range(2)]
            rs = [P.sb([128, 512], F32) for _ in range(2)]
            gst = [P.sb([128, 512], F32) for _ in range(2)]
            onb = [P.sb([128, 512], BF16) for _ in range(2)]
            order = [list(range(NCK)), [3, 2, 1, 0] + list(range(NCK - 1, 3, -1))]
            na = 0
            for h in range(8 if HDBG != 1 else 0):
                hs = slice(h * 128, (h + 1) * 128)
                P.dma("sp", q[:], QF[hs, :])
                P.dma("sp", v[:], VH[:, hs].rearrange("(c t) d -> t c d", t=64))
                for d in range(2):
                    P.dma("sp", A[:], FF[d][hs, :])
                    P.actv(Bq[:], A[:], AF.Ln)
                    if d == 0:
                        P.scan(Bq[:], rm[0][:], Bq[:], 0.0, ALU.mult, ALU.add)
                        bl = Bq[:, 63:T:64]
                    else:
                        P.scan(Bq[:, ::-1], rm[1][:, ::-1], Bq[:, ::-1], 0.0, ALU.mult, ALU.add)
                        bl = Bq[:, 0:T:64]
                    P.actv(ebl[d][:], bl, AF.Exp)
                    P.actv(Cb[:], Bq[:], AF.Exp)
                    P.stt("dve", qin[d][:], q[:], HG_SCALE, Cb[:], ALU.mult, ALU.mult)
                    P.ts("dve", A[:], A[:], -1.0, 1.0, ALU.mult, ALU.add)
                    P.actv(Cb[:], Bq[:], AF.Exp, scale=-1.0)
                    P.tt("dve", kin[d][:], A[:], Cb[:], ALU.mult)
                    P.tt("dve", Cb[:].rearrange("p (c t) -> p c t", t=64), bl.unsqueeze(2).to_broadcast([128, NCK, 64]),
                         Bq[:].rearrange("p (c t) -> p c t", t=64), ALU.subtract)
                    P.actv(Cb[:], Cb[:], AF.Exp)
                    P.tt("dve", kout[d][:], A[:], Cb[:], ALU.mult)
                    P.memset("pool", S[d][:], 0.0)
                    P.memset("pool", Sb[d][:], 0.0)
                P.memset("pool", oacc[:], 0.0)
                def issue_kt(step):
                    r = []
                    for d in range(2):
                        ci = order[d][step]
                        kp = P.bank()
                        P.mm(kp[0:64, 0:128], kout[d][:, ci * 64:ci * 64 + 64], C.ident[:])
                        r.append(kp)
                    return r

                def copy_kt(step, kps):
                    r = []
                    for d in range(2):
                        kk = ko[(2 * step + d) % 4]
                        P.copy("act", kk[:], kps[d][0:64, 0:128])
                        r.append(kk)
                    return r
                kks = copy_kt(0, issue_kt(0))
                for step in range(NCK):
                    cis = [order[d][step] for d in range(2)]
                    css = [slice(ci * 64, ci * 64 + 64) for ci in cis]
                    aps_ = []
                    for d in range(2):
                        ap_ = P.bank()
                        P.mm(ap_[0:64, 0:64], kin[d][:, css[d]], qin[d][:, css[d]])
                        aps_.append(ap_)
                    nkps = issue_kt(step + 1) if step + 1 < NCK else None
                    ams = []
                    for d in range(2):
                        am = attm[(2 * step + d) % 4]
                        P.tt("dve", am[:], aps_[d][0:64, 0:64], msk[d][:], ALU.mult)
                        ams.append(am)
                    nkks = copy_kt(step + 1, nkps) if nkps is not None else None
                    ops_ = []
                    for d in range(2):
                        op_ = P.bank()
                        P.mm(op_[:, 0:64], v[:, cis[d], :], ams[d][:], start=True, stop=False)
                        P.mm(op_[:, 0:64], Sb[d][:], qin[d][:, css[d]], start=False, stop=True)
                        ops_.append(op_)
                    dps = []
                    for d in range(2):
                        dp = P.bank()
                        P.mm(dp[:, 0:128], kks[d][:], v[:, cis[d], :])
                        dps.append(dp)
                    for d in range(2):
                        P.ts("dve", S[d][:], S[d][:], ebl[d][:, cis[d]:cis[d] + 1], None, ALU.mult)
                        P.tt("dve", S[d][:], dps[d][:, 0:128], S[d][:], ALU.add)
                        P.copy("act", Sb[d][:], S[d][:])
                    for d in range(2):
                        P.tt("dve", oacc[:, css[d]], ops_[d][:, 0:64], oacc[:, css[d]], ALU.add)
                    kks = nkks
                for ti, (t0, n) in enumerate(TILES):
                    s_, r_, g_, o_ = sq[ti % 2], rs[ti % 2], gst[ti % 2], onb[ti % 2]
                    P.dma("sp", g_[:, :n], GS[hs, t0:t0 + n])
                    P.actv(s_[:, :n], oacc[:, t0:t0 + n], AF.Square)
                    ps = P.bank()
                    P.mm(ps[:, :n], C.ones[:], s_[:, :n])
                    P.actv(r_[:, :n], ps[:, :n], AF.Sqrt, bias=C.epsb[:, 0:1], scale=1.0 / 128)
                    P.recip(r_[:, :n], r_[:, :n])
                    P.tt("dve", r_[:, :n], r_[:, :n], g_[:, :n], ALU.mult)
                    P.tt("dve", o_[:, :n], oacc[:, t0:t0 + n], r_[:, :n], ALU.mult)
                    P.dma("sp", OT[hs, t0:t0 + n], o_[:, :n])
    outproj_res(P, C, XT, OT, G["hg_w_out"], i)


MIXERS[2] = hgrn2_mixer


_NC_CACHE = {}


def kernel(**inputs):
    inp = {k: np.asarray(v) for k, v in inputs.items()}
    nb = inp["x"].shape[0]
    stages = []
    for i in range(4):
        stages += [("mixer", i), ("ffn", i)]
    stages.append(("final", None))
    if "nc" not in _NC_CACHE:
        _NC_CACHE["nc"] = build(stages)
    nc = _NC_CACHE["nc"]
    shared = host_shared_all(inp)
    maps = []
    for b in range(nb):
        m = dict(shared)
        m.update(host_common(inp, b))
        maps.append(m)
    res = run_bass_kernel_spmd(nc, maps, core_ids=list(range(nb)))
    out = np.stack([np.ascontiguousarray(res.results[b]["out"].T) for b in range(nb)], axis=0)
    return out.astype(np.float32)
```
